# Optimizing a Trainium2 kernel written in Bass

```python
import math
import jax
import jax.numpy as jnp
from jax import lax
import numpy as np

D_MODEL = 2048
BATCH = 4
SEQ = 4096
DEPTH = 4

GRID_W = 64
CTX_LEN = 256
N_EVEN = (DEPTH + 1) // 2
N_ODD = DEPTH // 2
D_FF = 4 * D_MODEL
N_MOD = 6
NORM_EPS = 1e-6

A_WIDTH = D_MODEL // 2
A_HEAD_DIM = 64
A_HEADS = A_WIDTH // (2 * A_HEAD_DIM)
ROPE_FREQS = A_HEAD_DIM // 4
ROPE_BASE = 10000.0
Q_BLOCK = 128

B_WIDTH = D_MODEL - A_WIDTH
B_BLOCKS = 8
B_BLOCK_DIM = B_WIDTH // B_BLOCKS
CONV_W = 4
CONV_LEFT = 2
LRU_C = 8.0

C_HEADS = 4
C_KEY_DIM = D_MODEL // 2 // C_HEADS
C_VAL_DIM = D_MODEL // C_HEADS
C_KEY_W = C_HEADS * C_KEY_DIM
C_GATE_RANK = 16
C_GATE_TAU = 16.0
C_CHUNK = 64

EVEN_IN = 3 * A_WIDTH + 2 * B_WIDTH
ODD_IN = 2 * C_KEY_W + 2 * D_MODEL + 2 * C_GATE_RANK

kernel_name = 'hybrid_diffattn_rglru_gla_dit'


def rmsnorm(x, g):
    xf = x.astype(jnp.float32)
    y = xf * lax.rsqrt(jnp.mean(xf * xf, axis=-1, keepdims=True) + NORM_EPS)
    return (y * g.astype(jnp.float32)).astype(x.dtype)


def modulate(h, shift, scale):
    return h * (1.0 + scale) + shift


def sq_relu_mlp(u, w1, w2):
    return jnp.square(jax.nn.relu(u @ w1)) @ w2


def grid_angles(rows):
    inv = ROPE_BASE ** (-jnp.arange(ROPE_FREQS, dtype=jnp.float32) / ROPE_FREQS)
    r = jnp.repeat(jnp.arange(rows, dtype=jnp.float32), GRID_W)
    col = jnp.tile(jnp.arange(GRID_W, dtype=jnp.float32), rows)
    return r[:, None] * inv, col[:, None] * inv


def rope_axis(x, ang):
    f = ang.shape[-1]
    x1, x2 = x[..., :f], x[..., f:]
    cs, sn = jnp.cos(ang)[:, None, :], jnp.sin(ang)[:, None, :]
    return jnp.concatenate([x1 * cs - x2 * sn, x2 * cs + x1 * sn], axis=-1)


def rope_2d(x, ang_r, ang_c):
    half = x.shape[-1] // 2
    return jnp.concatenate([rope_axis(x[..., :half], ang_r), rope_axis(x[..., half:], ang_c)], axis=-1)


def qk_pair(t):
    t = t.astype(jnp.float32).reshape(t.shape[0], t.shape[1], A_HEADS, 2, A_HEAD_DIM)
    return t[..., 0, :], t[..., 1, :]


def v_heads(t):
    return t.astype(jnp.float32).reshape(t.shape[0], t.shape[1], A_HEADS, 2 * A_HEAD_DIM)


def diff_softmax_core(q1, q2, k1, k2, v, lam):
    p1 = jax.nn.softmax(jnp.einsum('bqhd,bkhd->bhqk', q1, k1), axis=-1)
    p2 = jax.nn.softmax(jnp.einsum('bqhd,bkhd->bhqk', q2, k2), axis=-1)
    return jnp.einsum('bhqk,bkhe->bqhe', p1 - lam * p2, v)


def centred_dwconv(x, w, b):
    L = x.shape[1]
    xp = jnp.pad(x, ((0, 0), (CONV_LEFT, CONV_W - 1 - CONV_LEFT), (0, 0)))
    w = w.astype(jnp.float32)
    return sum(xp[:, j:j + L] * w[j] for j in range(CONV_W)) + b.astype(jnp.float32)


def block_diag(x, w, b):
    xb = x.reshape(x.shape[0], x.shape[1], B_BLOCKS, B_BLOCK_DIM)
    y = jnp.einsum('blni,nij->blnj', xb, w.astype(jnp.float32))
    return y.reshape(x.shape) + b.astype(jnp.float32)


def lru_coeffs(xc, wa, ba, wx, bx, lam):
    r = jax.nn.sigmoid(block_diag(xc, wa, ba))
    gi = jax.nn.sigmoid(block_diag(xc, wx, bx))
    log_a = -LRU_C * r * jax.nn.softplus(-lam.astype(jnp.float32))
    return jnp.exp(log_a), jnp.sqrt(-jnp.expm1(2.0 * log_a)) * gi * xc


def linear_scan(a, b, h0, reverse):
    idx = -1 if reverse else 0
    b = b.at[:, idx].add(a[:, idx] * h0)

    def combine(e1, e2):
        a1, b1 = e1
        a2, b2 = e2
        return a1 * a2, a2 * b1 + b2

    _, h = lax.associative_scan(combine, (a, b), axis=1, reverse=reverse)
    return h


def gla_chunked(q, k, v, log_a, s0):
    B, L, H, _ = q.shape
    dv = v.shape[-1]
    n = L // C_CHUNK

    def to_chunks(t):
        return t.reshape(B, n, C_CHUNK, H, t.shape[-1]).transpose(1, 0, 3, 2, 4)

    mask = jnp.tril(jnp.ones((C_CHUNK, C_CHUNK), dtype=bool))

    def step(S, inp):
        qc, kc, vc, gc = inp
        bcum = jnp.cumsum(gc, axis=2)
        qe = qc * jnp.exp(bcum)
        ke = kc * jnp.exp(-bcum)
        att = jnp.where(mask, jnp.einsum('bhtk,bhsk->bhts', qe, ke), 0.0)
        o = jnp.einsum('bhtk,bhkv->bhtv', qe, S) + jnp.einsum('bhts,bhsv->bhtv', att, vc)
        b_end = bcum[:, :, -1:]
        S = jnp.exp(b_end[:, :, 0])[..., None] * S + jnp.einsum('bhsk,bhsv->bhkv', kc * jnp.exp(b_end - bcum), vc)
        return S, o

    S, o = lax.scan(step, s0, (to_chunks(q), to_chunks(k), to_chunks(v), to_chunks(log_a)))
    return o.transpose(1, 0, 3, 2, 4).reshape(B, L, H, dv), S


def even_mixer(u, uc, ang_r, ang_c, w_in, w_out, lq1, lk1, lq2, lk2, subln_g, lam_init,
               conv_w, conv_b, wa, ba, wx, bx, lru_lam, need_ctx):
    f32 = jnp.float32
    B, L, _ = u.shape
    cuts = [A_WIDTH, 2 * A_WIDTH, 3 * A_WIDTH, 3 * A_WIDTH + B_WIDTH]
    q, k, v, gt, rec = jnp.split(u @ w_in, cuts, axis=-1)
    qc, kc, vc, gtc, recc = jnp.split(uc @ w_in, cuts, axis=-1)
    lam = (jnp.exp(jnp.sum(lq1.astype(f32) * lk1.astype(f32)))
           - jnp.exp(jnp.sum(lq2.astype(f32) * lk2.astype(f32))) + lam_init)
    scale = A_HEAD_DIM ** -0.5
    q1, q2 = (rope_2d(t, ang_r, ang_c) * scale for t in qk_pair(q))
    k1, k2 = (rope_2d(t, ang_r, ang_c) for t in qk_pair(k))
    k1c, k2c = qk_pair(kc)
    v_ctx = v_heads(vc)
    keys1 = jnp.concatenate([k1c, k1], axis=1)
    keys2 = jnp.concatenate([k2c, k2], axis=1)
    vals = jnp.concatenate([v_ctx, v_heads(v)], axis=1)
    nb = L // Q_BLOCK

    def to_blocks(t):
        return t.reshape(B, nb, Q_BLOCK, A_HEADS, A_HEAD_DIM).swapaxes(0, 1)

    o = lax.map(lambda qb: diff_softmax_core(qb[0], qb[1], keys1, keys2, vals, lam),
                (to_blocks(q1), to_blocks(q2)))
    o = o.swapaxes(0, 1).reshape(B, L, A_HEADS, 2 * A_HEAD_DIM)

    def a_post(t):
        return (rmsnorm(t, subln_g) * (1.0 - lam_init)).reshape(t.shape[0], t.shape[1], A_WIDTH)

    x_lat = centred_dwconv(rec.astype(f32), conv_w, conv_b)
    x_ctx = centred_dwconv(recc.astype(f32), conv_w, conv_b)
    h_lat, h_ctx = [], []
    for d in range(2):
        rev = d == 1
        a_c, b_c = lru_coeffs(x_ctx, wa[d], ba[d], wx[d], bx[d], lru_lam[d])
        hc_d = linear_scan(a_c, b_c, jnp.zeros((B, B_WIDTH), f32), rev)
        a_l, b_l = lru_coeffs(x_lat, wa[d], ba[d], wx[d], bx[d], lru_lam[d])
        h_lat.append(linear_scan(a_l, b_l, hc_d[:, 0] if rev else hc_d[:, -1], rev))
        h_ctx.append(hc_d)

    def merge(ya, gate, h):
        yb = jax.nn.gelu(gate.astype(f32)) * (h[0] + h[1])
        return jnp.concatenate([ya, yb], axis=-1).astype(u.dtype) @ w_out

    y = merge(a_post(o), gt, h_lat)
    if not need_ctx:
        return y, None
    q1c, q2c = (t * scale for t in qk_pair(qc))
    oc = diff_softmax_core(q1c, q2c, k1c, k2c, v_ctx, lam)
    return y, merge(a_post(oc), gtc, h_ctx)


def odd_mixer(u, uc, w_in, w_out, gate_w2, gate_b, norm_g, need_ctx):
    f32 = jnp.float32
    cuts = [C_KEY_W, 2 * C_KEY_W, 2 * C_KEY_W + D_MODEL, 2 * C_KEY_W + 2 * D_MODEL]

    def prep(p):
        B, L, _ = p.shape
        q, k, v, r, z = jnp.split(p.astype(f32), cuts, axis=-1)
        q = q.reshape(B, L, C_HEADS, C_KEY_DIM) * C_KEY_DIM ** -0.5
        k = k.reshape(B, L, C_HEADS, C_KEY_DIM)
        v = v.reshape(B, L, C_HEADS, C_VAL_DIM)
        z = z.reshape(B, L, 2, C_GATE_RANK)
        log_a = [(jax.nn.log_sigmoid(z[:, :, d] @ gate_w2[d].astype(f32) + gate_b[d].astype(f32))
                  / C_GATE_TAU).reshape(B, L, C_HEADS, C_KEY_DIM) for d in range(2)]
        return q, k, v, r, log_a

    q, k, v, r, la = prep(u @ w_in)
    qc, kc, vc, rc, lac = prep(uc @ w_in)
    B = u.shape[0]
    o_lat, o_ctx = [], []
    for d in range(2):
        f = (lambda t: t[:, ::-1]) if d == 1 else (lambda t: t)
        s0 = jnp.zeros((B, C_HEADS, C_KEY_DIM, C_VAL_DIM), f32)
        oc_d, s_ctx = gla_chunked(f(qc), f(kc), f(vc), f(lac[d]), s0)
        ol_d, _ = gla_chunked(f(q), f(k), f(v), f(la[d]), s_ctx)
        o_lat.append(f(ol_d))
        o_ctx.append(f(oc_d))

    def out(o, gate):
        y = rmsnorm(o, norm_g).reshape(o.shape[0], o.shape[1], D_MODEL) * jax.nn.silu(gate)
        return y.astype(u.dtype) @ w_out

    y = out(o_lat[0] + o_lat[1], r)
    if not need_ctx:
        return y, None
    return y, out(o_ctx[0] + o_ctx[1], rc)


def setup_inputs(seed: int = 0) -> dict:
    key = jax.random.key(seed)
    keys = iter(list(jax.random.split(key, 32)))
    f32 = jnp.float32

    def nrm(shape, scale=1.0):
        return scale * jax.random.normal(next(keys), shape, f32)

    def gain(shape):
        return 1.0 + nrm(shape, 0.05)

    x = nrm((BATCH, SEQ, D_MODEL))
    c = nrm((BATCH, D_MODEL))
    ctx = nrm((BATCH, CTX_LEN, D_MODEL))
    c_ctx = nrm((D_MODEL,))
    ada_w = nrm((DEPTH, D_MODEL, N_MOD * D_MODEL), 0.5 * D_MODEL ** -0.5)
    ada_b = nrm((DEPTH, N_MOD * D_MODEL), 0.02)
    norm1_g = gain((DEPTH, D_MODEL))
    norm2_g = gain((DEPTH, D_MODEL))
    mlp_w1 = nrm((DEPTH, D_MODEL, D_FF), D_MODEL ** -0.5)
    mlp_w2 = nrm((DEPTH, D_FF, D_MODEL), D_FF ** -0.5)
    ev_w_in = nrm((N_EVEN, D_MODEL, EVEN_IN), D_MODEL ** -0.5)
    ev_w_out = nrm((N_EVEN, D_MODEL, D_MODEL), D_MODEL ** -0.5)
    ev_lambda_q1 = nrm((N_EVEN, A_HEAD_DIM), 0.1)
    ev_lambda_k1 = nrm((N_EVEN, A_HEAD_DIM), 0.1)
    ev_lambda_q2 = nrm((N_EVEN, A_HEAD_DIM), 0.1)
    ev_lambda_k2 = nrm((N_EVEN, A_HEAD_DIM), 0.1)
    ev_subln_g = gain((N_EVEN, 2 * A_HEAD_DIM))
    ev_conv_w = nrm((N_EVEN, CONV_W, B_WIDTH), CONV_W ** -0.5)
    ev_conv_b = nrm((N_EVEN, B_WIDTH), 0.02)
    ev_lru_wa = nrm((N_EVEN, 2, B_BLOCKS, B_BLOCK_DIM, B_BLOCK_DIM), B_BLOCK_DIM ** -0.5)
    ev_lru_ba = nrm((N_EVEN, 2, B_WIDTH), 0.1)
    ev_lru_wx = nrm((N_EVEN, 2, B_BLOCKS, B_BLOCK_DIM, B_BLOCK_DIM), B_BLOCK_DIM ** -0.5)
    ev_lru_bx = nrm((N_EVEN, 2, B_WIDTH), 0.1)
    a_pow = jax.random.uniform(next(keys), (N_EVEN, 2, B_WIDTH), f32, 0.9, 0.999)
    a_base = a_pow ** (1.0 / LRU_C)
    ev_lru_lam = jnp.log(a_base) - jnp.log1p(-a_base)
    od_w_in = nrm((N_ODD, D_MODEL, ODD_IN), D_MODEL ** -0.5)
    od_w_out = nrm((N_ODD, D_MODEL, D_MODEL), D_MODEL ** -0.5)
    od_gate_w2 = nrm((N_ODD, 2, C_GATE_RANK, C_KEY_W), C_GATE_RANK ** -0.5)
    od_gate_b = nrm((N_ODD, 2, C_KEY_W), 0.1)
    od_norm_g = gain((N_ODD, C_VAL_DIM))
    final_g = gain((D_MODEL,))
    return {'x': x, 'c': c, 'ctx': ctx, 'c_ctx': c_ctx, 'ada_w': ada_w, 'ada_b': ada_b,
            'norm1_g': norm1_g, 'norm2_g': norm2_g, 'mlp_w1': mlp_w1, 'mlp_w2': mlp_w2,
            'ev_w_in': ev_w_in, 'ev_w_out': ev_w_out, 'ev_lambda_q1': ev_lambda_q1,
            'ev_lambda_k1': ev_lambda_k1, 'ev_lambda_q2': ev_lambda_q2, 'ev_lambda_k2': ev_lambda_k2,
            'ev_subln_g': ev_subln_g, 'ev_conv_w': ev_conv_w, 'ev_conv_b': ev_conv_b,
            'ev_lru_wa': ev_lru_wa, 'ev_lru_ba': ev_lru_ba, 'ev_lru_wx': ev_lru_wx,
            'ev_lru_bx': ev_lru_bx, 'ev_lru_lam': ev_lru_lam, 'od_w_in': od_w_in,
            'od_w_out': od_w_out, 'od_gate_w2': od_gate_w2, 'od_gate_b': od_gate_b,
            'od_norm_g': od_norm_g, 'final_g': final_g}


def reference(x, c, ctx, c_ctx, ada_w, ada_b, norm1_g, norm2_g, mlp_w1, mlp_w2,
              ev_w_in, ev_w_out, ev_lambda_q1, ev_lambda_k1, ev_lambda_q2, ev_lambda_k2,
              ev_subln_g, ev_conv_w, ev_conv_b, ev_lru_wa, ev_lru_ba, ev_lru_wx, ev_lru_bx,
              ev_lru_lam, od_w_in, od_w_out, od_gate_w2, od_gate_b, od_norm_g, final_g):
    ROWS = x.shape[1] // GRID_W
    ang_r, ang_c = grid_angles(ROWS)
    s_lat = jax.nn.silu(c)
    s_ctx = jax.nn.silu(c_ctx)
    h, hc = x, ctx
    for i in range(DEPTH):
        need_ctx = i < DEPTH - 1
        m = jnp.split((s_lat @ ada_w[i] + ada_b[i])[:, None, :], N_MOD, axis=-1)
        mc = jnp.split(s_ctx @ ada_w[i] + ada_b[i], N_MOD, axis=-1)
        u = modulate(rmsnorm(h, norm1_g[i]), m[0], m[1])
        uc = modulate(rmsnorm(hc, norm1_g[i]), mc[0], mc[1])
        j = i // 2
        if i % 2 == 0:
            lam_init = 0.8 - 0.6 * math.exp(-0.3 * i)
            y, yc = even_mixer(u, uc, ang_r, ang_c, ev_w_in[j], ev_w_out[j],
                               ev_lambda_q1[j], ev_lambda_k1[j], ev_lambda_q2[j], ev_lambda_k2[j],
                               ev_subln_g[j], lam_init, ev_conv_w[j], ev_conv_b[j],
                               ev_lru_wa[j], ev_lru_ba[j], ev_lru_wx[j], ev_lru_bx[j],
                               ev_lru_lam[j], need_ctx)
        else:
            y, yc = odd_mixer(u, uc, od_w_in[j], od_w_out[j], od_gate_w2[j], od_gate_b[j],
                              od_norm_g[j], need_ctx)
        h = h + m[2] * y
        h = h + m[5] * sq_relu_mlp(modulate(rmsnorm(h, norm2_g[i]), m[3], m[4]), mlp_w1[i], mlp_w2[i])
        if need_ctx:
            hc = hc + mc[2] * yc
            hc = hc + mc[5] * sq_relu_mlp(modulate(rmsnorm(hc, norm2_g[i]), mc[3], mc[4]),
                                          mlp_w1[i], mlp_w2[i])
    return rmsnorm(h, final_g)
```

```python
import math
from collections import defaultdict

import numpy as np
import concourse.bass as bass
import concourse.mybir as mybir
from concourse.bass_utils import run_bass_kernel_spmd

F32 = mybir.dt.float32
BF16 = mybir.dt.bfloat16
AF = mybir.ActivationFunctionType
ALU = mybir.AluOpType
AX = mybir.AxisListType


class Prog:
    COMPUTE = ("pe", "act", "dve", "pool")
    SEM_EPOCH = 20000
    _uid = [0]

    def __init__(self, nc):
        self.nc = nc
        self.ops = []
        self.n_dma_sems = 0

    def op(self, eng, fn, reads=(), writes=(), dma_key=None):
        self.ops.append(dict(eng=eng, fn=fn, reads=tuple(reads), writes=tuple(writes),
                             dma_key=dma_key))

    def dma(self, q, out, in_, reads=(), writes=(), key=None, **kw):
        assert key is not None
        self.op(q, lambda e: e.dma_start(out=out, in_=in_, **kw), reads, writes, dma_key=key)

    def emit(self):
        nc = self.nc
        ops = self.ops
        n = len(ops)
        last_writer = {}
        readers = defaultdict(list)
        deps = [None] * n
        for i, o in enumerate(ops):
            d = set()
            for r in o["reads"]:
                if r in last_writer:
                    d.add(last_writer[r])
            for w in o["writes"]:
                if w in last_writer:
                    d.add(last_writer[w])
                for rd in readers[w]:
                    d.add(rd)
            d.discard(i)
            if o["dma_key"] is not None:
                pk = ("__dmakey__", o["dma_key"])
                if pk in last_writer:
                    d.add(last_writer[pk])
                last_writer[pk] = i
            if o["eng"] == "pe" and o["dma_key"] is None:
                d = {j for j in d if not (ops[j]["eng"] == "pe" and ops[j]["dma_key"] is None)}
            deps[i] = d
            for w in o["writes"]:
                last_writer[w] = i
                readers[w] = []
            for r in o["reads"]:
                if r not in o["writes"]:
                    readers[r].append(i)
        needed = [False] * n
        for d in deps:
            for j in d:
                needed[j] = True
        marker = [None] * n
        eng_cnt = defaultdict(int)
        eng_sems = defaultdict(list)
        dma_sem = {}
        dma_cnt = defaultdict(int)
        for i, o in enumerate(ops):
            if o["dma_key"] is not None:
                k = o["dma_key"]
                if k not in dma_sem:
                    Prog._uid[0] += 1
                    dma_sem[k] = nc.alloc_semaphore("dq%d" % Prog._uid[0])
                dma_cnt[k] += 16
                marker[i] = (dma_sem[k], dma_cnt[k], 16)
            elif needed[i]:
                e = o["eng"]
                c = eng_cnt[e]
                ep = c // self.SEM_EPOCH
                if ep >= len(eng_sems[e]):
                    Prog._uid[0] += 1
                    eng_sems[e].append(nc.alloc_semaphore("s_%s_%d" % (e, Prog._uid[0])))
                eng_cnt[e] = c + 1
                marker[i] = (eng_sems[e][ep], c % self.SEM_EPOCH + 1, 1)
        self.n_dma_sems = len(dma_sem)
        final_dma = [(dma_sem[k], dma_cnt[k]) for k in dma_sem]
        by_eng = defaultdict(list)
        for i, o in enumerate(ops):
            by_eng[o["eng"]].append(i)

        def run(engname, eng):
            seen = {}
            for i in by_eng.get(engname, []):
                o = ops[i]
                for j in sorted(deps[i]):
                    sem, val, _ = marker[j]
                    sid = id(sem)
                    if seen.get(sid, 0) < val:
                        eng.wait_ge(sem, val)
                        seen[sid] = val
                inst = o["fn"](eng)
                if marker[i] is not None:
                    sem, val, inc = marker[i]
                    inst.then_inc(sem, inc)
            if engname == "sp":
                for sem, val in final_dma:
                    eng.wait_ge(sem, val)

        with nc.Block() as block:
            @block.sync
            def _(e):
                run("sp", e)

            @block.tensor
            def _(e):
                run("pe", e)

            @block.scalar
            def _(e):
                run("act", e)

            @block.vector
            def _(e):
                run("dve", e)

            @block.gpsimd
            def _(e):
                run("pool", e)
        return dict(n_ops=n, n_dma_sems=len(dma_sem),
                    per_eng={k: len(v) for k, v in by_eng.items()})


D = 2048
DFF = 8192
CTX = 256
NMOD = 6
EPS = 1e-6
EV_IN = 5120
OD_IN = 6176
GRID_W = 64


class Deferred:
    def __init__(self, passthrough=()):
        self.q = []
        self.passthrough = set(passthrough)

    def _k(self, x):
        return x if x in self.passthrough else ("L", x)

    def op(self, eng, fn, reads=(), writes=(), dma_key=None):
        self.q.append(("op", (eng, fn, [self._k(x) for x in reads], [self._k(x) for x in writes]),
                       dict(dma_key=None if dma_key is None else self._k(dma_key))))

    def dma(self, q, out, in_, reads=(), writes=(), key=None, **kw):
        self.q.append(("dma", (q, out, in_), dict(reads=[self._k(x) for x in reads],
                                                   writes=[self._k(x) for x in writes], key=self._k(key), **kw)))

    def pull(self, ph, n):
        last_pe = False
        while (n > 0 or last_pe) and self.q:
            kind, a, k = self.q.pop(0)
            getattr(ph.p, kind)(*a, **k)
            last_pe = (kind == "op" and a[0] == "pe")
            n -= 1


class Cfg:
    def __init__(self, L, depth):
        self.L = L
        self.depth = depth
        self.T = CTX + L
        self.NE = (depth + 1) // 2
        self.NO = depth // 2
        self.blocks = [(0, CTX, True)] + [(CTX + 512 * j, 512, False) for j in range(L // 512)]
        self.NKT = self.T // 128


class Phase:
    _ctr = [0]

    def __init__(self, nc, name):
        self.nc = nc
        Phase._ctr[0] += 1
        self.name = "%s%d" % (name, Phase._ctr[0])
        self.stats = None

    def __enter__(self):
        self.cm = self.nc.cleanup_on_exit()
        self.cm.__enter__()
        self.p = Prog(self.nc)
        return self

    def __exit__(self, et, ev, tb):
        if et is not None:
            return False
        self.stats = self.p.emit()
        self.nc.all_engine_barrier()
        self.cm.__exit__(None, None, None)
        return False

    def sb(self, name, shape, dt):
        return self.nc.alloc_sbuf_tensor("%s_%s" % (self.name, name), list(shape), dt)

    def ps(self, name, shape=(128, 512), dt=F32):
        return self.nc.alloc_psum_tensor("%s_%s" % (self.name, name), list(shape), dt)

    def op(self, *a, **k):
        self.p.op(*a, **k)

    def dma(self, *a, **k):
        self.p.dma(*a, **k)


def wtile_view(wd, idx, kct, cw):
    return wd[idx].rearrange("p (k c) -> p k c", c=cw) if False else wd[idx]


def convert_list(src, dst, K, ncols, kct, cw):
    out = []
    nkg = K // (kct * 128)
    nblk = (ncols + cw - 1) // cw
    for kg in range(nkg):
        for blk in range(nblk):
            c0 = blk * cw
            w = min(cw, ncols - c0)
            s = src[kg * kct * 128:(kg + 1) * kct * 128, c0:c0 + w].rearrange("(k p) j -> p k j", p=128)
            d = dst[kg * nblk + blk][:, :, 0:w]
            out.append((d, s))
    return out


def build_program(cfg, dbg=None):
    nc = bass.Bass("TRN2", target_bir_lowering=False)
    L, T, DEPTH, NE, NO = cfg.L, cfg.T, cfg.depth, cfg.NE, cfg.NO
    NKT = cfg.NKT

    def din(name, shape, dt=F32):
        return nc.dram_tensor(name, list(shape), dt, kind="ExternalInput")

    x_d = din("x", [L, D])
    ctx_d = din("ctx", [CTX, D])
    svec_d = din("svec", [128, 16, 2])
    adaw_d = din("ada_w", [DEPTH, D, NMOD * D])
    adab_d = din("ada_b_l", [128, DEPTH, 96])
    gains_d = din("gains", [128, 2 * DEPTH + 1, 16])
    w1_d = din("mlp_w1", [DEPTH, D, DFF])
    w2_d = din("mlp_w2", [DEPTH, DFF, D])
    evin_d = din("ev_w_in", [NE, D, EV_IN])
    evout_d = din("ev_w_out", [NE, D, D])
    odin_d = din("od_w_in", [max(NO, 1), D, OD_IN])
    odout_d = din("od_w_out", [max(NO, 1), D, D])
    lamv_d = din("lamv", [1, NE, 4, 64])
    subln_d = din("subln", [128, NE])
    convw_d = din("convw", [128, NE, 8, 5])
    lruwa_d = din("lru_wa", [NE, 2, 8, 128, 128])
    lruwx_d = din("lru_wx", [NE, 2, 8, 128, 128])
    lruv_d = din("lru_v", [128, NE, 2, 8, 3])
    gw2_d = din("gate_w2", [max(NO, 1), 2, 16, 1024])
    gb_d = din("gate_b_l", [128, max(NO, 1), 2, 8])
    odg_d = din("od_norm_g_l", [128, max(NO, 1), 4])
    ident_d = din("ident", [128, 128])
    rt_d = din("RT", [128, 128])
    rope_d = din("rope", [4, 128, T])
    tri_d = din("tri", [2, 128, 128])
    out_d = nc.dram_tensor("out", [L, D], F32, kind="ExternalOutput")

    hT_d = nc.dram_tensor("hT", [D, T], F32)
    yT_d = nc.dram_tensor("yT", [D, T], BF16)
    projA_d = nc.dram_tensor("projA", [2048, T], BF16)
    projB_d = nc.dram_tensor("projB", [4128, T], F32)
    vTok_d = nc.dram_tensor("vTok", [T, 2048], BF16)
    wb_in = [nc.dram_tensor("wb_in%d" % i, [13 if i % 2 else 10, 128, 16, 512], BF16) for i in range(DEPTH)]
    wb_out = [nc.dram_tensor("wb_out%d" % i, [4, 128, 16, 512], BF16) for i in range(DEPTH)]
    wb_1 = [nc.dram_tensor("wb_1_%d" % i, [16, 128, 16, 512], BF16) for i in range(DEPTH)]
    wb_2 = [nc.dram_tensor("wb_2_%d" % i, [32, 128, 32, 128], BF16) for i in range(DEPTH)]
    dbg_d = {}
    if dbg:
        for nm in dbg:
            src = dict(hT=hT_d, yT=yT_d, projA=projA_d, projB=projB_d, vTok=vTok_d)[nm]
            dbg_d[nm] = (nc.dram_tensor("dbg_" + nm, list(src.shape), src.dtype, kind="ExternalOutput"), src)

    ident32 = nc.alloc_sbuf_tensor("ident32", [128, 128], F32)
    identbf = nc.alloc_sbuf_tensor("identbf", [128, 128], BF16)
    ones32 = nc.alloc_sbuf_tensor("ones32", [128, 128], F32)
    onesbf = nc.alloc_sbuf_tensor("onesbf", [128, 128], BF16)
    rt32 = nc.alloc_sbuf_tensor("rt32", [128, 128], F32)
    modS = nc.alloc_sbuf_tensor("modS", [128, DEPTH, 96, 2], F32)
    gains = nc.alloc_sbuf_tensor("gains_sb", [128, 2 * DEPTH + 1, 16], F32)
    zero1 = nc.alloc_sbuf_tensor("zero1", [128, 1], F32)
    epsc = nc.alloc_sbuf_tensor("epsc", [128, 1], F32)

    stats = []

    conv = []
    for i in range(DEPTH):
        j = i // 2
        if i % 2 == 0:
            lst = convert_list(evin_d[j], wb_in[i], D, EV_IN, 16, 512) + convert_list(evout_d[j], wb_out[i], D, D, 16, 512)
        else:
            lst = convert_list(odin_d[j], wb_in[i], D, OD_IN, 16, 512) + convert_list(odout_d[j], wb_out[i], D, D, 16, 512)
        lst += convert_list(w1_d[i], wb_1[i], D, DFF, 16, 512) + convert_list(w2_d[i], wb_2[i], DFF, D, 32, 128)
        conv.append(lst)
    conv_pos = [0] * DEPTH
    conv_ctr = [0]

    def emit_conv(ph, i, n, first=False):
        if i >= DEPTH:
            return
        while n > 0 and conv_pos[i] < len(conv[i]):
            d, s_ = conv[i][conv_pos[i]]
            conv_pos[i] += 1
            conv_ctr[0] += 1
            ph.dma("pool", d, s_, key=("cv", conv_ctr[0] % 8))
            n -= 1

    with Phase(nc, "cv") as ph:
        ph.dma("sp", ident32[:], ident_d.ap(), writes=["i32"], key="i32")
        ph.dma("sp", rt32[:], rt_d.ap(), writes=["rt"], key="rt")
        ph.dma("sp", gains[:], gains_d.ap(), writes=["gains"], key="gains")
        ph.op("dve", lambda e: e.tensor_copy(out=identbf[:], in_=ident32[:]), ["i32"], ["ibf"])
        ph.op("dve", lambda e: e.memset(ones32[:], 1.0), [], ["o32"])
        ph.op("dve", lambda e: e.memset(onesbf[:], 1.0), [], ["obf"])
        ph.op("dve", lambda e: e.memset(zero1[:], 0.0), [], ["z1"])
        ph.op("dve", lambda e: e.memset(epsc[:], EPS), [], ["epsc"])
        emit_conv(ph, 0, 10, first=True)
    stats.append(("cv", ph.stats))

    with Phase(nc, "mod") as ph:
        s32 = ph.sb("s32", [128, 16, 2], F32)
        sact = ph.sb("sact", [128, 16, 2], F32)
        adab = ph.sb("adab", [128, DEPTH, 96], F32)
        wts = [ph.sb("w%d" % k, [128, 16, 512], F32) for k in range(2)]
        rows = [ph.sb("row%d" % k, [2, 512], F32) for k in range(2)]
        prow = [ph.ps("prow%d" % k) for k in range(2)]
        pT = [ph.ps("pT%d" % k, [128, 192]) for k in range(2)]
        ph.dma("sp", s32[:], svec_d.ap(), writes=["s32"], key="s32")
        ph.dma("sp", adab[:], adab_d.ap(), writes=["adab"], key="adab")
        ph.op("act", lambda e: e.activation(out=sact[:], in_=s32[:], func=AF.Silu), ["s32"], ["sact"])
        n = 0
        for i in range(DEPTH):
            pTi = pT[i % 2]
            for cb in range(24):
                sl = n % 2
                n += 1
                src = adaw_d[i][:, cb * 512:(cb + 1) * 512].rearrange("(k p) j -> p k j", p=128)
                ph.dma("sp", wts[sl][:], src, writes=[("w", sl)], key=("w", sl))

                def mm(e, sl=sl):
                    for kc in range(16):
                        r = e.matmul(prow[sl][0:2, :], lhsT=sact[:, kc, :], rhs=wts[sl][:, kc, :],
                                     start=(kc == 0), stop=(kc == 15))
                    return r
                ph.op("pe", mm, ["sact", ("w", sl)], [("prow", sl)])
                ph.op("act", lambda e, sl=sl: e.activation(out=rows[sl][:], in_=prow[sl][0:2, :], func=AF.Copy),
                      [("prow", sl)], [("row", sl)])

                def tr(e, sl=sl, cb=cb, pTi=pTi):
                    for q in range(4):
                        c = cb * 4 + q
                        r = e.transpose(pTi[:, 2 * c:2 * c + 2], rows[sl][0:2, q * 128:(q + 1) * 128],
                                        ident32[0:2, 0:2])
                    return r
                ph.op("pe", tr, [("row", sl)], [("pT", i % 2)])
                emit_conv(ph, 0, 1)
            mv = modS[:, i, :, :]
            ph.op("dve", lambda e, pTi=pTi, mv=mv, i=i: e.tensor_tensor(
                out=mv, in0=pTi[:, :].rearrange("p (c r) -> p c r", r=2),
                in1=adab[:, i, :].unsqueeze(2).broadcast_to([128, 96, 2]), op=ALU.add),
                [("pT", i % 2), "adab"], [("modS", i)])
            for (j, g) in ((1, 2 * i), (4, 2 * i + 1)):
                sv = modS[:, i, j * 16:(j + 1) * 16, :]
                ph.op("dve", lambda e, sv=sv: e.tensor_scalar_add(out=sv, in0=sv, scalar1=1.0),
                      [("modS", i)], [("modS", i)])
                ph.op("dve", lambda e, sv=sv, g=g: e.tensor_tensor(
                    out=sv, in0=sv, in1=gains[:, g, :].unsqueeze(2).broadcast_to([128, 16, 2]), op=ALU.mult),
                    [("modS", i)], [("modS", i)])
        emit_conv(ph, 0, 10 ** 6)
    stats.append(("mod", ph.stats))

    def modcol(i, j, c, r):
        return modS[:, i, j * 16 + c, r:r + 1]

    def token_pass(li_prev, li_next):
        first = li_prev is None
        last = li_next is None
        with Phase(nc, "tp") as ph:
            hblk = ph.sb("hblk", [128, 16, 512], F32)
            ybuf = ph.sb("ybuf", [128, 16, 512], BF16)
            uT = ph.sb("uT", [128, 16, 512], BF16)
            hid = ph.sb("hid", [128, 32, 512], BF16)
            wt = [ph.sb("wt%d" % k, [128, 8192], BF16) for k in range(3)]
            rstd = ph.sb("rstd", [128, 512], F32)
            tmpc = [ph.sb("tmpc%d" % k, [128, 512], F32) for k in range(2)]
            sqf = [ph.sb("sqf%d" % k, [128, 512], F32) for k in range(2)]
            q32 = [ph.sb("q32_%d" % k, [128, 512], F32) for k in range(2)]
            tab = ph.sb("tab", [128, 4, 512], F32)
            t1 = [ph.sb("t1_%d" % k, [128, 512], F32) for k in range(2)]
            t2 = [ph.sb("t2_%d" % k, [128, 512], F32) for k in range(2)]
            stgA = [ph.sb("stgA%d" % k, [128, 512], BF16) for k in range(2)]
            stgB = [ph.sb("stgB%d" % k, [128, 512], F32) for k in range(2)]
            xtok = [ph.sb("xtok%d" % k, [128, 2048], F32) for k in range(2)]
            pl = [ph.ps("pl%d" % k) for k in range(3)]
            pstat = ph.ps("pstat")
            prope = ph.ps("prope")
            ptm = [ph.ps("ptm%d" % k) for k in range(2)]
            ptr = ph.ps("ptr")
            cnt = defaultdict(int)

            def nxt(name, mod):
                v = cnt[name] % mod
                cnt[name] += 1
                return v

            def load_w(wd_tile, nelem):
                sl = nxt("wt", 3)
                ph.dma("sp", wt[sl][:, 0:nelem], wd_tile.rearrange("p k c -> p (k c)"),
                       writes=[("wt", sl)], key=("wt", sl))
                return sl

            def norm(N, gcol, shcol, out_fn):
                ph.op("act", lambda e: e.activation(out=ybuf[:, :, 0:N], in_=hblk[:, :, 0:N], func=AF.Square),
                      ["hblk"], ["ybuf"])

                def mm(e):
                    for c in range(16):
                        r = e.matmul(pstat[:, 0:N], lhsT=onesbf[:], rhs=ybuf[:, c, 0:N],
                                     start=(c == 0), stop=(c == 15))
                    return r
                ph.op("pe", mm, ["ybuf"], ["pstat"])
                ph.op("act", lambda e: e.activation(out=rstd[:, 0:N], in_=pstat[:, 0:N], func=AF.Sqrt,
                                                    scale=1.0 / D, bias=epsc[:, 0:1]), ["pstat"], ["rstd"])
                ph.op("dve", lambda e: e.reciprocal(out=rstd[:, 0:N], in_=rstd[:, 0:N]), ["rstd"], ["rstd"])
                for c in range(16):
                    sl = nxt("tmpc", 2)
                    ph.op("dve", lambda e, c=c, sl=sl: e.tensor_tensor(
                        out=tmpc[sl][:, 0:N], in0=hblk[:, c, 0:N], in1=rstd[:, 0:N], op=ALU.mult),
                        ["hblk", "rstd"], [("tmpc", sl)])
                    out_fn(c, sl, gcol(c), shcol(c))

            def norm_to_uT(N, gcol, shcol):
                def out_fn(c, sl, g, sh):
                    ph.op("act", lambda e: e.activation(out=uT[:, c, 0:N], in_=tmpc[sl][:, 0:N],
                                                        func=AF.Identity, scale=g, bias=sh),
                          [("tmpc", sl)], ["uT"])
                norm(N, gcol, shcol, out_fn)

            def linear_fm(xT, xkey, wd, wblocks, N, epi, kct=16, cw=512, koff=0):
                for (ti, oc0, ncol) in wblocks:
                    sl = load_w(wd[ti], kct * cw)
                    wv = wt[sl][:, 0:kct * cw].rearrange("p (k c) -> p k c", c=cw)
                    for m in range((ncol + 127) // 128):
                        mw = min(128, ncol - m * 128)
                        b = nxt("pl", 3)

                        def mm(e, wv=wv, m=m, mw=mw, b=b):
                            for kc in range(kct):
                                r = e.matmul(pl[b][0:mw, 0:N], lhsT=wv[:, kc, m * 128:m * 128 + mw],
                                             rhs=xT[:, koff + kc, 0:N], start=(kc == 0), stop=(kc == kct - 1))
                            return r
                        ph.op("pe", mm, [xkey, ("wt", sl)], [("pl", b)])
                        epi(oc0 + m, b, mw)

            def linear_tm(wd, wblocks, N, epi):
                for (ti, cb0) in wblocks:
                    sl = load_w(wd[ti], 8192)
                    wv = wt[sl][:, :].rearrange("p (k c) -> p k c", c=512)
                    for tt in range(N // 128):
                        b = nxt("ptm", 2)

                        def mm(e, wv=wv, tt=tt, b=b):
                            for kc in range(16):
                                r = e.matmul(ptm[b][:, :], lhsT=uT[:, kc, tt * 128:(tt + 1) * 128],
                                             rhs=wv[:, kc, :], start=(kc == 0), stop=(kc == 15))
                            return r
                        ph.op("pe", mm, ["uT", ("wt", sl)], [("ptm", b)])
                        epi(cb0, tt, b)

            def residual_epi(N, gate_fn):
                def epi(oc, b, mw):
                    g = gate_fn(oc)
                    ph.op("dve", lambda e: e.scalar_tensor_tensor(
                        out=hblk[:, oc, 0:N], in0=pl[b][:, 0:N], scalar=g, in1=hblk[:, oc, 0:N],
                        op0=ALU.mult, op1=ALU.add), [("pl", b), "hblk"], ["hblk"])
                return epi

            def do_block(t0, N, is_ctx):
                r = 1 if is_ctx else 0
                if last and is_ctx:
                    return
                if first:
                    src = ctx_d if is_ctx else x_d
                    r0 = 0 if is_ctx else t0 - CTX
                    for tt in range(N // 128):
                        xs = nxt("xtok", 2)
                        ph.dma("sp", xtok[xs][:], src[r0 + tt * 128:r0 + (tt + 1) * 128, :],
                               writes=[("xtok", xs)], key=("xtok", xs))
                        for c0 in range(0, 16, 4):
                            def trp(e, xs=xs, c0=c0):
                                for q in range(4):
                                    rr = e.transpose(ptr[:, q * 128:(q + 1) * 128],
                                                     xtok[xs][:, (c0 + q) * 128:(c0 + q + 1) * 128], ident32[:])
                                return rr
                            ph.op("pe", trp, [("xtok", xs)], ["ptr"])
                            ph.op("act", lambda e, c0=c0, tt=tt: e.activation(
                                out=hblk[:, c0:c0 + 4, tt * 128:(tt + 1) * 128],
                                in_=ptr[:, :].rearrange("p (q t) -> p q t", t=128), func=AF.Copy),
                                ["ptr"], ["hblk"])
                else:
                    ph.dma("sp", hblk[:, :, 0:N], hT_d[:, t0:t0 + N].rearrange("(c p) t -> p c t", p=128),
                           writes=["hblk"], key="hblk")
                if not first:
                    i = li_prev
                    ph.dma("sp", ybuf[:, :, 0:N], yT_d[:, t0:t0 + N].rearrange("(c p) t -> p c t", p=128),
                           writes=["ybuf"], key="ybuf")
                    linear_fm(ybuf, "ybuf", wb_out[i], [(k, 4 * k, 512) for k in range(4)], N,
                              residual_epi(N, lambda oc: modcol(i, 2, oc, r)))
                    norm_to_uT(N, lambda c: modcol(i, 4, c, r), lambda c: modcol(i, 3, c, r))
                    for half in range(2):
                        def hid_epi(oc, b, mw, half=half):
                            j = oc - half * 32
                            s = nxt("sqf", 2)
                            ph.op("act", lambda e: e.activation(out=sqf[s][:, 0:N], in_=pl[b][:, 0:N],
                                                                func=AF.Square), [("pl", b)], [("sqf", s)])
                            ph.op("dve", lambda e: e.scalar_tensor_tensor(
                                out=hid[:, j, 0:N], in0=pl[b][:, 0:N], scalar=0.0, in1=sqf[s][:, 0:N],
                                op0=ALU.is_gt, op1=ALU.mult), [("pl", b), ("sqf", s)], ["hid"])
                        linear_fm(uT, "uT", wb_1[i], [(half * 8 + k, half * 32 + 4 * k, 512) for k in range(8)],
                                  N, hid_epi)
                        linear_fm(hid, "hid", wb_2[i], [(half * 16 + f, f, 128) for f in range(16)], N,
                                  residual_epi(N, lambda oc: modcol(i, 5, oc, r)), kct=32, cw=128)
                if not last:
                    ph.dma("pool", hT_d[:, t0:t0 + N].rearrange("(c p) t -> p c t", p=128), hblk[:, :, 0:N],
                           reads=["hblk"], writes=[], key="hst")
                if not last:
                    i = li_next
                    norm_to_uT(N, lambda c: modcol(i, 1, c, r), lambda c: modcol(i, 0, c, r))
                    if i % 2 == 0:
                        ph.dma("sp", tab[:, :, 0:N], rope_d[:, :, t0:t0 + N].rearrange("f p t -> p f t"),
                               writes=["tab"], key="tab")

                        def qk_epi(oc, b, mw):
                            isk = 1 if oc >= 8 else 0
                            s = nxt("q32", 2)
                            ph.op("act", lambda e: e.activation(out=q32[s][:, 0:N], in_=pl[b][:, 0:N],
                                                                func=AF.Copy), [("pl", b)], [("q32", s)])
                            ph.op("pe", lambda e: e.matmul(prope[:, 0:N], lhsT=rt32[:], rhs=q32[s][:, 0:N],
                                                           start=True, stop=True), [("q32", s)], ["prope"])
                            ph.op("dve", lambda e: e.tensor_tensor(out=t1[s][:, 0:N], in0=q32[s][:, 0:N],
                                                                   in1=tab[:, 2 * isk, 0:N], op=ALU.mult),
                                  [("q32", s), "tab"], [("t1", s)])
                            ph.op("dve", lambda e: e.tensor_tensor(out=t2[s][:, 0:N], in0=prope[:, 0:N],
                                                                   in1=tab[:, 2 * isk + 1, 0:N], op=ALU.mult),
                                  ["prope", "tab"], [("t2", s)])
                            ph.op("pool", lambda e: e.tensor_tensor(out=stgA[s][:, 0:N], in0=t1[s][:, 0:N],
                                                                    in1=t2[s][:, 0:N], op=ALU.add),
                                  [("t1", s), ("t2", s)], [("stgA", s)])
                            ph.dma("pool", projA_d[oc * 128:(oc + 1) * 128, t0:t0 + N], stgA[s][:, 0:N],
                                   reads=[("stgA", s)], key=("stgA", s))

                        def f32_epi_rows(row_of):
                            def epi(oc, b, mw):
                                s = nxt("stgB", 2)
                                ph.op("act", lambda e: e.activation(out=stgB[s][0:mw, 0:N], in_=pl[b][0:mw, 0:N],
                                                                    func=AF.Copy), [("pl", b)], [("stgB", s)])
                                r0 = row_of(oc)
                                ph.dma("pool", projB_d[r0:r0 + mw, t0:t0 + N], stgB[s][0:mw, 0:N],
                                       reads=[("stgB", s)], key=("stgB", s))
                            return epi

                        def v_epi(ncolblk):
                            def epi(cb0, tt, b):
                                s = nxt("stgA", 2)
                                ph.op("act", lambda e: e.activation(out=stgA[s][:, :], in_=ptm[b][:, :],
                                                                    func=AF.Copy), [("ptm", b)], [("stgA", s)])
                                ph.dma("pool", vTok_d[t0 + tt * 128:t0 + (tt + 1) * 128, cb0 * 512:(cb0 + 1) * 512],
                                       stgA[s][:, :], reads=[("stgA", s)], key=("stgA", s))
                            return epi
                        linear_fm(uT, "uT", wb_in[i], [(k, 4 * k, 512) for k in range(4)], N, qk_epi)
                        linear_tm(wb_in[i], [(4, 0), (5, 1)], N, v_epi(2))
                        linear_fm(uT, "uT", wb_in[i], [(k, 4 * k, 512) for k in range(6, 10)], N,
                                  f32_epi_rows(lambda oc: (oc - 24) * 128))
                    else:
                        def f32_epi_rows(row_of):
                            def epi(oc, b, mw):
                                s = nxt("stgB", 2)
                                ph.op("act", lambda e: e.activation(out=stgB[s][0:mw, 0:N], in_=pl[b][0:mw, 0:N],
                                                                    func=AF.Copy), [("pl", b)], [("stgB", s)])
                                r0 = row_of(oc)
                                ph.dma("pool", projB_d[r0:r0 + mw, t0:t0 + N], stgB[s][0:mw, 0:N],
                                       reads=[("stgB", s)], key=("stgB", s))
                            return epi

                        def v_epi(cb0, tt, b):
                            s = nxt("stgA", 2)
                            ph.op("act", lambda e: e.activation(out=stgA[s][:, :], in_=ptm[b][:, :],
                                                                func=AF.Copy), [("ptm", b)], [("stgA", s)])
                            ph.dma("pool", vTok_d[t0 + tt * 128:t0 + (tt + 1) * 128, cb0 * 512:(cb0 + 1) * 512],
                                   stgA[s][:, :], reads=[("stgA", s)], key=("stgA", s))
                        linear_fm(uT, "uT", wb_in[i], [(k, 4 * k, 512) for k in range(4)], N,
                                  f32_epi_rows(lambda oc: oc * 128))
                        linear_tm(wb_in[i], [(4 + k, k) for k in range(4)], N, v_epi)
                        linear_fm(uT, "uT", wb_in[i], [(k, 4 * k, 512) for k in range(8, 12)] + [(12, 48, 32)], N,
                                  f32_epi_rows(lambda oc: 2048 + (oc - 32) * 128))
                else:
                    fg = 2 * DEPTH

                    def out_fn(c, sl, g, sh):
                        ph.op("act", lambda e: e.activation(out=hblk[:, c, 0:N], in_=tmpc[sl][:, 0:N],
                                                            func=AF.Identity, scale=g, bias=sh),
                              [("tmpc", sl), "hblk"], ["hblk"])
                    norm(N, lambda c: gains[:, fg, c:c + 1], lambda c: zero1[:, 0:1], out_fn)
                    for tt in range(N // 128):
                        xs = nxt("xtok", 2)
                        for c0 in range(0, 16, 4):
                            def trp(e, c0=c0, tt=tt):
                                for q in range(4):
                                    rr = e.transpose(ptr[:, q * 128:(q + 1) * 128],
                                                     hblk[:, c0 + q, tt * 128:(tt + 1) * 128], ident32[:])
                                return rr
                            ph.op("pe", trp, ["hblk"], ["ptr"])
                            ph.op("act", lambda e, c0=c0, xs=xs: e.activation(
                                out=xtok[xs][:, c0 * 128:(c0 + 4) * 128], in_=ptr[:, :], func=AF.Copy),
                                ["ptr"], [("xtok", xs)])
                        r0 = t0 - CTX + tt * 128
                        ph.dma("pool", out_d[r0:r0 + 128, :], xtok[xs][:], reads=[("xtok", xs)], key=("xtok", xs))
            for blk in cfg.blocks:
                do_block(*blk)
        stats.append(("tp", ph.stats))

    def attention(li):
        j = li // 2
        lam_init = 0.8 - 0.6 * math.exp(-0.3 * li)
        need_ctx = li < DEPTH - 1
        with Phase(nc, "at") as ph:
            kT = [ph.sb("kT%d" % k, [128, T], BF16) for k in range(2)]
            vt = [ph.sb("vt%d" % k, [128, NKT, 128], BF16) for k in range(2)]
            qT = [ph.sb("qT%d" % k, [128, 512], BF16) for k in range(2)]
            r1 = ph.sb("r1", [128, 512], F32)
            r2 = ph.sb("r2", [128, 512], F32)
            oa = ph.sb("oa", [128, 512], F32)
            ob = ph.sb("ob", [128, 512], F32)
            osb = ph.sb("osb", [128, 512], F32)
            sq = ph.sb("sq", [128, 512], F32)
            rs = ph.sb("rs", [128, 512], F32)
            yv = ph.sb("yv", [128, 512], F32)
            ystg = [ph.sb("ystg%d" % k, [128, 512], BF16) for k in range(2)]
            lv = ph.sb("lv", [1, 4, 64], F32)
            prod = ph.sb("prod", [1, 2, 64], F32)
            ss2 = ph.sb("ss2", [1, 2], F32)
            e2 = ph.sb("e2", [1, 2], F32)
            lb = ph.sb("lb", [128, 2], F32)
            neglam = ph.sb("neglam", [128, 1], F32)
            sg = ph.sb("sg", [128, 1], F32)
            subl = ph.sb("subl", [128, NE], F32)
            NSB = 2
            sbk = [ph.ps("sbk%d" % k, [128, 1024]) for k in range(NSB)]
            o1, o2, d1, pst = ph.ps("o1"), ph.ps("o2"), ph.ps("d1"), ph.ps("pst")
            acc2 = ph.sb("acc2", [128, 512], F32)
            pp = [ph.sb("pp%d" % k, [128, 2, 512], BF16) for k in range(3)]
            cnt = defaultdict(int)

            def nxt(name, mod):
                v = cnt[name] % mod
                cnt[name] += 1
                return v

            ph.dma("sp", lv[:], lamv_d[0:1, j], writes=["lv"], key="lv")
            ph.dma("sp", subl[:], subln_d.ap(), writes=["subl"], key="subl")
            ph.op("dve", lambda e: e.tensor_tensor(out=prod[0:1, 0, :], in0=lv[0:1, 0, :], in1=lv[0:1, 1, :],
                                                   op=ALU.mult), ["lv"], ["prod"])
            ph.op("dve", lambda e: e.tensor_tensor(out=prod[0:1, 1, :], in0=lv[0:1, 2, :], in1=lv[0:1, 3, :],
                                                   op=ALU.mult), ["lv", "prod"], ["prod"])
            ph.op("dve", lambda e: e.reduce_sum(out=ss2[0:1, :], in_=prod[0:1, :, :], axis=AX.X), ["prod"], ["ss2"])
            ph.op("act", lambda e: e.activation(out=e2[0:1, :], in_=ss2[0:1, :], func=AF.Exp), ["ss2"], ["e2"])
            ph.op("pe", lambda e: e.matmul(pst[:, 0:2], lhsT=ones32[0:1, :], rhs=e2[0:1, 0:2], start=True, stop=True),
                  ["e2"], ["pst"])
            ph.op("act", lambda e: e.activation(out=lb[:], in_=pst[:, 0:2], func=AF.Copy), ["pst"], ["lb"])
            ph.op("dve", lambda e: e.tensor_tensor(out=neglam[:], in0=lb[:, 1:2], in1=lb[:, 0:1], op=ALU.subtract),
                  ["lb"], ["neglam"])
            ph.op("dve", lambda e: e.tensor_scalar_add(out=neglam[:], in0=neglam[:], scalar1=-lam_init),
                  ["neglam"], ["neglam"])
            ph.op("dve", lambda e: e.tensor_scalar_mul(out=sg[:], in0=subl[:, j:j + 1], scalar1=1.0 - lam_init),
                  ["subl"], ["sg"])

            lrq = Deferred(passthrough=("pst",))
            rglru_record(li, ph, lrq, pst)
            pend = {"stages": None}

            def run_stage(k):
                st = pend["stages"]
                if st is not None and st[k] is not None:
                    f_ = st[k]
                    st[k] = None
                    f_()

            def flush():
                for k in range(3):
                    run_stage(k)

            def do_qblock(h, hs, t0, N, is_ctx):
                qs = nxt("qT", 2)
                ph.dma("sp", qT[qs][:, 0:N], projA_d[h * 128:(h + 1) * 128, t0:t0 + N],
                       writes=[("qT", qs)], key=("qT", qs))
                kts = [0, 1] if is_ctx else list(range(NKT))
                nk = len(kts)

                def QK(idx):
                    kt = kts[idx]
                    b = idx % NSB

                    def f(e):
                        e.matmul(sbk[b][:, 0:N], lhsT=kT[hs][0:64, kt * 128:(kt + 1) * 128], rhs=qT[qs][0:64, 0:N],
                                 start=True, stop=True)
                        return e.matmul(sbk[b][:, 512:512 + N], lhsT=kT[hs][64:128, kt * 128:(kt + 1) * 128],
                                        rhs=qT[qs][64:128, 0:N], start=True, stop=True)
                    ph.op("pe", f, [("kT", hs), ("qT", qs)], [("sbk", b)])

                def EXP(idx):
                    b = idx % NSB
                    s = idx % 3
                    ph.op("act", lambda e: e.activation(
                        out=pp[s][:, :, 0:N], in_=sbk[b][:, :].rearrange("p (m n) -> p m n", m=2)[:, :, 0:N],
                        func=AF.Exp), [("sbk", b)], [("pp", s)])

                def PV(idx):
                    kt = kts[idx]
                    s = idx % 3
                    st, sp_ = (idx == 0), (idx == nk - 1)

                    def f(e):
                        e.matmul(o1[:, 0:N], lhsT=vt[hs][:, kt, :], rhs=pp[s][:, 0, 0:N], start=st, stop=sp_)
                        e.matmul(d1[:, 0:N], lhsT=onesbf[:], rhs=pp[s][:, 0, 0:N], start=st, stop=sp_)
                        return e.matmul(o2[:, 0:N], lhsT=vt[hs][:, kt, :], rhs=pp[s][:, 1, 0:N], start=st, stop=sp_)
                    ph.op("pe", f, [("vt", hs), ("pp", s)], ["acc"])
                    if st:
                        ph.op("dve", lambda e: e.tensor_copy(out=acc2[:, 0:N], in_=pp[s][:, 1, 0:N]), [("pp", s)], ["acc2"])
                    else:
                        ph.op("dve", lambda e: e.tensor_tensor(out=acc2[:, 0:N], in0=pp[s][:, 1, 0:N], in1=acc2[:, 0:N],
                                                               op=ALU.add), [("pp", s), "acc2"], ["acc2"])

                QK(0)
                if nk > 1:
                    QK(1)
                for idx in range(nk):
                    EXP(idx)
                    if idx + 2 < nk:
                        QK(idx + 2)
                    PV(idx)
                    if idx % 3 == 1:
                        lrq.pull(ph, 1)
                    if idx == min(2, nk - 1):
                        run_stage(0)
                    if idx == min(6, nk - 1):
                        run_stage(1)
                    if idx == min(8, nk - 1):
                        run_stage(2)
                ph.op("pe", lambda e: e.matmul(pst[:, 0:N], lhsT=ones32[:], rhs=acc2[:, 0:N], start=True, stop=True),
                      ["acc2"], ["pst"])
                ph.op("dve", lambda e: e.reciprocal(out=r1[:, 0:N], in_=d1[:, 0:N]), ["acc"], ["r1"])
                ph.op("dve", lambda e: e.tensor_tensor(out=oa[:, 0:N], in0=o1[:, 0:N], in1=r1[:, 0:N], op=ALU.mult),
                      ["acc", "r1"], ["oa"])
                ph.op("dve", lambda e: e.reciprocal(out=r2[:, 0:N], in_=pst[:, 0:N]), ["pst"], ["r2"])
                ph.op("dve", lambda e: e.tensor_tensor(out=ob[:, 0:N], in0=o2[:, 0:N], in1=r2[:, 0:N], op=ALU.mult),
                      ["acc", "r2"], ["ob"])

                def stage0():
                    ph.op("dve", lambda e: e.scalar_tensor_tensor(out=osb[:, 0:N], in0=ob[:, 0:N], scalar=neglam[:, 0:1],
                                                                  in1=oa[:, 0:N], op0=ALU.mult, op1=ALU.add),
                          ["oa", "ob", "neglam"], ["osb"])
                    ph.op("dve", lambda e: e.tensor_tensor(out=sq[:, 0:N], in0=osb[:, 0:N], in1=osb[:, 0:N], op=ALU.mult),
                          ["osb"], ["sq"])

                def stage1():
                    ph.op("pe", lambda e: e.matmul(pst[:, 0:N], lhsT=ones32[:], rhs=sq[:, 0:N], start=True, stop=True),
                          ["sq"], ["pst"])
                    ph.op("act", lambda e: e.activation(out=rs[:, 0:N], in_=pst[:, 0:N], func=AF.Ln,
                                                        scale=1.0 / 128, bias=epsc[:, 0:1]), ["pst"], ["rs"])
                    ph.op("act", lambda e: e.activation(out=rs[:, 0:N], in_=rs[:, 0:N], func=AF.Exp, scale=-0.5),
                          ["rs"], ["rs"])

                def stage2():
                    ph.op("dve", lambda e: e.tensor_tensor(out=yv[:, 0:N], in0=osb[:, 0:N], in1=rs[:, 0:N], op=ALU.mult),
                          ["osb", "rs"], ["yv"])
                    ys = nxt("ystg", 2)
                    ph.op("dve", lambda e: e.tensor_scalar_mul(out=ystg[ys][:, 0:N], in0=yv[:, 0:N], scalar1=sg[:, 0:1]),
                          ["yv", "sg"], [("ystg", ys)])
                    ph.dma("pool", yT_d[h * 128:(h + 1) * 128, t0:t0 + N], ystg[ys][:, 0:N],
                           reads=[("ystg", ys)], key=("ystg", ys))
                flush()
                pend["stages"] = [stage0, stage1, stage2]

            for h in range(8):
                hs = h % 2
                ph.dma("sp", kT[hs][:], projA_d[1024 + h * 128:1024 + (h + 1) * 128, :],
                       writes=[("kT", hs)], key=("kT", hs))
                ph.dma("sp", vt[hs][:], vTok_d[:, h * 128:(h + 1) * 128].rearrange("(kt p) e -> p kt e", p=128),
                       writes=[("vt", hs)], key=("vt", hs))
                for (t0, N, is_ctx) in cfg.blocks:
                    if is_ctx and not need_ctx:
                        continue
                    do_qblock(h, hs, t0, N, is_ctx)
                    emit_conv(ph, li + 1, 1)
            flush()
            lrq.pull(ph, 10 ** 9)
            emit_conv(ph, li + 1, 10 ** 6)
        stats.append(("at", ph.stats))

    def rglru_record(li, pha, ph, bank):
        j = li // 2
        if True:
            B = [pha.sb("B%d" % k, [128, T], F32) for k in range(6)]
            B.append(B[3])
            ystg = pha.sb("lystg", [128, T], BF16)
            wa = pha.sb("lwa", [128, 2, 8, 128], F32)
            wx = pha.sb("lwx", [128, 2, 8, 128], F32)
            cw = pha.sb("lcw", [128, 8, 5], F32)
            lv = pha.sb("llv", [128, 2, 8, 3], F32)
            negc = pha.sb("lnegc", [128, 2, 8], F32)
            etmp = pha.sb("letmp", [128, 2, 8], F32)
            pa = [bank, bank]
            px = [bank, bank]
            for d in range(2):
                ph.dma("sp", wa[:, d, :, :], lruwa_d[j, d].rearrange("n i o -> i n o"), writes=["wa"], key=("wa", d))
                ph.dma("sp", wx[:, d, :, :], lruwx_d[j, d].rearrange("n i o -> i n o"), writes=["wx"], key=("wx", d))
            ph.dma("sp", cw[:], convw_d[:, j], writes=["cw"], key="cw")
            ph.dma("sp", lv[:], lruv_d[:, j], writes=["lv"], key="lv")
            ph.op("act", lambda e: e.activation(out=etmp[:], in_=lv[:, :, :, 2], func=AF.Exp, scale=-1.0),
                  ["lv"], ["etmp"])
            ph.op("act", lambda e: e.activation(out=etmp[:], in_=etmp[:], func=AF.Ln, bias=1.0), ["etmp"], ["etmp"])
            ph.op("dve", lambda e: e.tensor_scalar_mul(out=negc[:], in0=etmp[:], scalar1=-8.0), ["etmp"], ["negc"])
            nlv = pha.sb("lnlv", [128, 2, 8, 3], F32)
            ph.op("dve", lambda e: e.tensor_scalar_mul(out=nlv[:], in0=lv[:], scalar1=-1.0), ["lv"], ["nlv"])
            segs = [(0, CTX), (CTX, T)]
            tb = [(t0, N) for (t0, N, _) in cfg.blocks]

            def chunk(n):
                rec, xx, gt, t3, gi, hs_, hh = B
                ph.dma("sp", rec[:], projB_d[1024 + n * 128:1024 + (n + 1) * 128, :], writes=["B0"], key="B0")
                ph.dma("sp", gt[:], projB_d[n * 128:(n + 1) * 128, :], writes=["B2"], key="B2")
                ph.op("act", lambda e: e.activation(out=xx[:], in_=rec[:], func=AF.Identity, scale=cw[:, n, 2:3],
                                                    bias=cw[:, n, 4:5]), ["B0", "cw"], ["B1"])
                for (a0, a1) in segs:
                    for (tap, sh) in ((0, -2), (1, -1), (3, 1)):
                        lo, hi = max(a0, a0 - sh), min(a1, a1 - sh)
                        ph.op("dve", lambda e, lo=lo, hi=hi, sh=sh, tap=tap: e.scalar_tensor_tensor(
                            out=xx[:, lo:hi], in0=rec[:, lo + sh:hi + sh], scalar=cw[:, n, tap:tap + 1],
                            in1=xx[:, lo:hi], op0=ALU.mult, op1=ALU.add), ["B0", "B1", "cw"], ["B1"])
                ph.op("dve", lambda e: e.tensor_tensor(out=t3[:], in0=gt[:], in1=gt[:], op=ALU.mult), ["B2"], ["B3"])
                ph.op("dve", lambda e: e.tensor_scalar(out=t3[:], in0=t3[:], scalar1=0.044715, scalar2=1.0,
                                                       op0=ALU.mult, op1=ALU.add), ["B3"], ["B3"])
                ph.op("dve", lambda e: e.tensor_tensor(out=t3[:], in0=t3[:], in1=gt[:], op=ALU.mult),
                      ["B3", "B2"], ["B3"])
                ph.op("act", lambda e: e.activation(out=t3[:], in_=t3[:], func=AF.Exp,
                                                    scale=-2.0 * math.sqrt(2.0 / math.pi)), ["B3"], ["B3"])
                ph.op("act", lambda e: e.activation(out=t3[:], in_=t3[:], func=AF.Ln, bias=1.0), ["B3"], ["B3"])
                ph.op("act", lambda e: e.activation(out=t3[:], in_=t3[:], func=AF.Exp, scale=-1.0), ["B3"], ["B3"])
                ph.op("dve", lambda e: e.tensor_tensor(out=gt[:], in0=t3[:], in1=gt[:], op=ALU.mult),
                      ["B3", "B2"], ["B2"])
                for d in range(2):
                    for bi, (t0, N) in enumerate(tb):
                        pb = bi % 2
                        ph.op("pe", lambda e, t0=t0, N=N, pb=pb, d=d: e.matmul(
                            pa[pb][:, 0:N], lhsT=wa[:, d, n, :], rhs=xx[:, t0:t0 + N], start=True, stop=True),
                            ["wa", "B1"], ["pst"])
                        ph.op("act", lambda e, t0=t0, N=N, pb=pb, d=d: e.activation(
                            out=rec[:, t0:t0 + N], in_=pa[pb][:, 0:N], func=AF.Exp, scale=-1.0, bias=nlv[:, d, n, 0:1]),
                            ["pst", "nlv"], ["B0"])
                        ph.op("pe", lambda e, t0=t0, N=N, pb=pb, d=d: e.matmul(
                            px[pb][:, 0:N], lhsT=wx[:, d, n, :], rhs=xx[:, t0:t0 + N], start=True, stop=True),
                            ["wx", "B1"], ["pst"])
                        ph.op("act", lambda e, t0=t0, N=N, pb=pb, d=d: e.activation(
                            out=gi[:, t0:t0 + N], in_=px[pb][:, 0:N], func=AF.Exp, scale=-1.0, bias=nlv[:, d, n, 1:2]),
                            ["pst", "nlv"], ["B4"])
                    for (buf, key) in ((rec, "B0"), (gi, "B4")):
                        ph.op("act", lambda e, buf=buf: e.activation(out=buf[:], in_=buf[:], func=AF.Ln, bias=1.0),
                              [key], [key])
                        ph.op("act", lambda e, buf=buf: e.activation(out=buf[:], in_=buf[:], func=AF.Exp, scale=-1.0),
                              [key], [key])
                    ph.op("act", lambda e, d=d: e.activation(out=rec[:], in_=rec[:], func=AF.Exp,
                                                             scale=negc[:, d, n:n + 1]), ["B0", "negc"], ["B0"])
                    ph.op("dve", lambda e: e.tensor_tensor(out=t3[:], in0=rec[:], in1=rec[:], op=ALU.mult),
                          ["B0"], ["B3"])
                    ph.op("act", lambda e: e.activation(out=t3[:], in_=t3[:], func=AF.Ln, scale=-1.0, bias=1.0),
                          ["B3"], ["B3"])
                    ph.op("act", lambda e: e.activation(out=t3[:], in_=t3[:], func=AF.Exp, scale=0.5), ["B3"], ["B3"])
                    ph.op("dve", lambda e: e.tensor_tensor(out=gi[:], in0=gi[:], in1=xx[:], op=ALU.mult),
                          ["B4", "B1"], ["B4"])
                    ph.op("dve", lambda e: e.tensor_tensor(out=gi[:], in0=gi[:], in1=t3[:], op=ALU.mult),
                          ["B4", "B3"], ["B4"])
                    if d == 0:
                        ph.op("dve", lambda e: e.tensor_tensor_scan(out=hs_[:], data0=rec[:], data1=gi[:],
                                                                    initial=0.0, op0=ALU.mult, op1=ALU.add),
                              ["B0", "B4"], ["B5"])
                    else:
                        ph.op("dve", lambda e: e.tensor_tensor_scan(
                            out=hh[:, 0:CTX][:, ::-1], data0=rec[:, 0:CTX][:, ::-1], data1=gi[:, 0:CTX][:, ::-1],
                            initial=0.0, op0=ALU.mult, op1=ALU.add), ["B0", "B4"], ["B3"])
                        ph.op("dve", lambda e: e.tensor_tensor_scan(
                            out=hh[:, CTX:T][:, ::-1], data0=rec[:, CTX:T][:, ::-1], data1=gi[:, CTX:T][:, ::-1],
                            initial=hh[:, 0:1], op0=ALU.mult, op1=ALU.add), ["B0", "B4", "B3"], ["B3"])
                        ph.op("dve", lambda e: e.tensor_tensor(out=hs_[:], in0=hs_[:], in1=hh[:], op=ALU.add),
                              ["B5", "B3"], ["B5"])
                ph.op("dve", lambda e: e.tensor_tensor(out=ystg[:], in0=hs_[:], in1=gt[:], op=ALU.mult),
                      ["B5", "B2"], ["ystg"])
                ph.dma("pool", yT_d[1024 + n * 128:1024 + (n + 1) * 128, :], ystg[:], reads=["ystg"], key="ystg")

            for n in range(8):
                chunk(n)

    NCH = T // 128
    qe_d = nc.dram_tensor("qe", [2, 1024, T], BF16)
    ke_d = nc.dram_tensor("ke", [2, 1024, T], BF16)
    kend_d = nc.dram_tensor("kend", [2, 4, T, 256], BF16)
    ebend_d = nc.dram_tensor("ebend", [2, 8, 128, NCH], F32)
    o0_d = nc.dram_tensor("o0", [2048, T], F32)

    def gla_prep(li):
        j = li // 2
        tb = [(t0, N) for (t0, N, _) in cfg.blocks]
        with Phase(nc, "gp") as ph:
            qv = ph.sb("qv", [128, T], F32)
            kv = ph.sb("kv", [128, T], F32)
            spb = ph.sb("spb", [128, T], F32)
            bcs = ph.sb("bcs", [128, T], F32)
            E = ph.sb("E", [128, T], F32)
            tmp = ph.sb("tmp", [128, T], F32)
            mask = ph.sb("mask", [128, T], F32)
            qeb = ph.sb("qeb", [128, T], BF16)
            keb = ph.sb("keb", [128, T], BF16)
            knb = ph.sb("knb", [128, T], BF16)
            kstg = ph.sb("kstg", [128, NCH, 128], BF16)
            zt = ph.sb("zt", [16, T], F32)
            gw = ph.sb("gw", [16, 1024], F32)
            gb = ph.sb("gb", [128, 2, 8], F32)
            negb = ph.sb("negb", [128, 2, 8], F32)
            ebe = ph.sb("ebe", [128, NCH], F32)
            pg = [ph.ps("pg%d" % k) for k in range(2)]
            pt = [ph.ps("pt%d" % k, [128, 512], BF16) for k in range(2)]
            ph.dma("sp", gb[:], gb_d[:, j], writes=["gb"], key="gb")
            ph.op("dve", lambda e: e.tensor_scalar_mul(out=negb[:], in0=gb[:], scalar1=-1.0), ["gb"], ["negb"])
            cnt = defaultdict(int)

            def do_cc(d, cc):
                h, kc = cc // 2, cc % 2
                for bi, (t0, N) in enumerate(tb):
                    pb = bi % 2
                    ph.op("pe", lambda e, t0=t0, N=N, pb=pb: e.matmul(
                        pg[pb][:, 0:N], lhsT=gw[0:16, cc * 128:(cc + 1) * 128], rhs=zt[0:16, t0:t0 + N],
                        start=True, stop=True), ["gw", "zt"], [("pg", pb)])
                    ph.op("act", lambda e, t0=t0, N=N, pb=pb: e.activation(
                        out=spb[:, t0:t0 + N], in_=pg[pb][:, 0:N], func=AF.Exp, scale=-1.0,
                        bias=negb[:, d, cc:cc + 1]), [("pg", pb), "negb"], ["spb"])
                ph.op("act", lambda e: e.activation(out=spb[:], in_=spb[:], func=AF.Ln, bias=1.0), ["spb"], ["spb"])
                if d == 0:
                    ph.op("dve", lambda e: e.tensor_tensor_scan(out=bcs[:], data0=mask[:], data1=spb[:], initial=0.0,
                                                                op0=ALU.mult, op1=ALU.add), ["mask", "spb"], ["bcs"])
                    bend = bcs[:, 127::128]
                else:
                    ph.op("dve", lambda e: e.tensor_tensor_scan(out=bcs[:, ::-1], data0=mask[:, ::-1],
                                                                data1=spb[:, ::-1], initial=0.0,
                                                                op0=ALU.mult, op1=ALU.add), ["mask", "spb"], ["bcs"])
                    bend = bcs[:, 0::128]
                ph.dma("sp", qv[:], projB_d[cc * 128:(cc + 1) * 128, :], writes=["qv"], key="qv")
                ph.dma("sp", kv[:], projB_d[1024 + cc * 128:1024 + (cc + 1) * 128, :], writes=["kv"], key="kv")
                ph.op("act", lambda e: e.activation(out=E[:], in_=bcs[:], func=AF.Exp, scale=-1.0 / 16), ["bcs"], ["E"])
                ph.op("dve", lambda e: e.scalar_tensor_tensor(out=qeb[:], in0=qv[:], scalar=1.0 / 16, in1=E[:],
                                                              op0=ALU.mult, op1=ALU.mult), ["qv", "E"], ["qeb"])
                ph.dma("pool", qe_d[d, cc * 128:(cc + 1) * 128, :], qeb[:], reads=["qeb"], key="qeb")
                ph.op("act", lambda e: e.activation(out=E[:], in_=bcs[:], func=AF.Exp, scale=1.0 / 16), ["bcs", "E"], ["E"])
                ph.op("dve", lambda e: e.tensor_tensor(out=keb[:], in0=kv[:], in1=E[:], op=ALU.mult),
                      ["kv", "E"], ["keb"])
                ph.dma("pool", ke_d[d, cc * 128:(cc + 1) * 128, :], keb[:], reads=["keb"], key="keb")
                ph.op("dve", lambda e: e.tensor_tensor(
                    out=tmp[:].rearrange("p (c t) -> p c t", t=128), in0=bcs[:].rearrange("p (c t) -> p c t", t=128),
                    in1=bend.unsqueeze(2).broadcast_to([128, NCH, 128]), op=ALU.subtract), ["bcs"], ["tmp"])
                ph.op("act", lambda e: e.activation(out=tmp[:], in_=tmp[:], func=AF.Exp, scale=1.0 / 16), ["tmp"], ["tmp"])
                ph.op("dve", lambda e: e.tensor_tensor(out=knb[:], in0=kv[:], in1=tmp[:], op=ALU.mult),
                      ["kv", "tmp"], ["knb"])
                ph.op("act", lambda e: e.activation(out=ebe[:], in_=bend, func=AF.Exp, scale=-1.0 / 16), ["bcs"], ["ebe"])
                ph.dma("pool", ebend_d[d, cc], ebe[:], reads=["ebe"], key="ebe")
                for c0 in range(0, NCH, 4):
                    nq = min(4, NCH - c0)
                    pb = cnt["pt"] % 2
                    cnt["pt"] += 1

                    def trp(e, c0=c0, nq=nq, pb=pb):
                        for q in range(nq):
                            rr = e.transpose(pt[pb][:, q * 128:(q + 1) * 128],
                                             knb[:, (c0 + q) * 128:(c0 + q + 1) * 128], identbf[:])
                        return rr
                    ph.op("pe", trp, ["knb"], [("pt", pb)])
                    ph.op("act", lambda e, c0=c0, nq=nq, pb=pb: e.activation(
                        out=kstg[:, c0:c0 + nq, :], in_=pt[pb][:, 0:nq * 128].rearrange("p (q k) -> p q k", k=128),
                        func=AF.Copy), [("pt", pb)], ["kstg"])
                ph.dma("pool", kend_d[d, h][:, kc * 128:(kc + 1) * 128].rearrange("(c s) k -> s c k", s=128),
                       kstg[:], reads=["kstg"], key="kstg")

            for d in range(2):
                ph.dma("sp", zt[:], projB_d[4096 + d * 16:4096 + (d + 1) * 16, :], writes=["zt"], key="zt")
                ph.dma("sp", gw[:], gw2_d[j, d], writes=["gw"], key="gw")
                ph.op("pool", lambda e: e.memset(mask[:], 1.0), [], ["mask"])
                zc = 0 if d == 0 else 127
                ph.op("pool", lambda e, zc=zc: e.memset(mask[:, zc::128], 0.0), ["mask"], ["mask"])
                for cc in range(8):
                    do_cc(d, cc)
                    emit_conv(ph, li + 1, 5)
            emit_conv(ph, li + 1, 10 ** 6)
        stats.append(("gp", ph.stats))

    def gla_rec(li):
        j = li // 2
        need_ctx = li < DEPTH - 1
        with Phase(nc, "gr") as ph:
            S32 = [ph.sb("S32_%d" % h, [128, 2, 512], F32) for h in range(4)]
            Sbf = [ph.sb("Sbf_%d" % h, [128, 2, 512], BF16) for h in range(4)]
            qes = [ph.sb("qes%d" % h, [128, 2, 512], BF16) for h in range(4)]
            kes = [ph.sb("kes%d" % h, [128, 2, 512], BF16) for h in range(4)]
            kns = [ph.sb("kns%d" % h, [128, 4, 256], BF16) for h in range(4)]
            vs = [ph.sb("vs%d" % h, [128, 4, 512], BF16) for h in range(4)]
            osb = [ph.sb("osb%d" % h, [128, 4, 512], F32) for h in range(4)]
            ebs = [ph.sb("ebs%d" % h, [128, 2, NCH], F32) for h in range(4)]
            atts = [ph.sb("att%d" % h, [128, 128], BF16) for h in range(4)]
            o0s = ph.sb("o0s", [128, 4, 512], F32)
            rg = ph.sb("rg", [128, 4, 512], F32)
            sqb = ph.sb("sqb", [128, 4, 512], BF16)
            rstd = ph.sb("rstd", [128, 512], F32)
            tm1 = ph.sb("tm1", [128, 512], F32)
            ystg = ph.sb("ystg", [128, 4, 512], BF16)
            tri = ph.sb("tri", [128, 2, 128], F32)
            odg = ph.sb("odg", [128, 4], F32)
            patt = [ph.ps("patt%d" % k) for k in range(2)]
            po = [ph.ps("po%d" % k) for k in range(2)]
            psu = [ph.ps("psu%d" % k) for k in range(4)]
            ph.dma("sp", tri[:], tri_d.ap().rearrange("d s t -> s d t"), writes=["tri"], key="tri")
            ph.dma("sp", odg[:], odg_d[:, j], writes=["odg"], key="odg")
            cnt = defaultdict(int)

            def nxt(name, mod):
                v = cnt[name] % mod
                cnt[name] += 1
                return v

            def chunk_step(d, h, c, lc):
                pa_ = nxt("patt", 2)
                H = ("h", h)

                def att(e):
                    e.matmul(patt[pa_][:, 0:128], lhsT=kes[h][:, 0, lc * 128:(lc + 1) * 128],
                             rhs=qes[h][:, 0, lc * 128:(lc + 1) * 128], start=True, stop=False)
                    return e.matmul(patt[pa_][:, 0:128], lhsT=kes[h][:, 1, lc * 128:(lc + 1) * 128],
                                    rhs=qes[h][:, 1, lc * 128:(lc + 1) * 128], start=False, stop=True)
                ph.op("pe", att, [("kes", h), ("qes", h)], [("patt", pa_)])
                ph.op("dve", lambda e: e.tensor_tensor(out=atts[h][:], in0=patt[pa_][:, 0:128], in1=tri[:, d, :],
                                                       op=ALU.mult), [("patt", pa_), "tri"], [("att", h)])
                pb_ = nxt("po", 2)

                def omm(e):
                    for m in range(4):
                        e.matmul(po[pb_][:, m * 128:(m + 1) * 128], lhsT=Sbf[h][:, 0, m * 128:(m + 1) * 128],
                                 rhs=qes[h][:, 0, lc * 128:(lc + 1) * 128], start=True, stop=False)
                        e.matmul(po[pb_][:, m * 128:(m + 1) * 128], lhsT=Sbf[h][:, 1, m * 128:(m + 1) * 128],
                                 rhs=qes[h][:, 1, lc * 128:(lc + 1) * 128], start=False, stop=False)
                        rr = e.matmul(po[pb_][:, m * 128:(m + 1) * 128], lhsT=vs[h][:, lc, m * 128:(m + 1) * 128],
                                      rhs=atts[h][:], start=False, stop=True)
                    return rr
                ph.op("pe", omm, [("Sbf", h), ("qes", h), ("vs", h), ("att", h)], [("po", pb_)])
                ph.op("act", lambda e: e.activation(
                    out=osb[h][:, :, lc * 128:(lc + 1) * 128],
                    in_=po[pb_][:, :].rearrange("p (m t) -> p m t", t=128), func=AF.Copy), [("po", pb_)], [("osb", h)])
                ps_ = nxt("psu", 2)

                def smm(e):
                    e.matmul(psu[2 * ps_][:, :], lhsT=kns[h][:, lc, 0:128], rhs=vs[h][:, lc, :], start=True, stop=True)
                    return e.matmul(psu[2 * ps_ + 1][:, :], lhsT=kns[h][:, lc, 128:256], rhs=vs[h][:, lc, :],
                                    start=True, stop=True)
                ph.op("pe", smm, [("kns", h), ("vs", h)], [("psu", ps_)])
                for kc in range(2):
                    ph.op("dve", lambda e, kc=kc: e.scalar_tensor_tensor(
                        out=S32[h][:, kc, :], in0=S32[h][:, kc, :], scalar=ebs[h][:, kc, c:c + 1],
                        in1=psu[2 * ps_ + kc][:, :], op0=ALU.mult, op1=ALU.add),
                        [("S32", h, kc), ("psu", ps_), ("ebs", h)], [("S32", h, kc)])
                ph.op("act", lambda e: e.activation(out=Sbf[h][:, 0, :], in_=S32[h][:, 0, :], func=AF.Copy),
                      [("S32", h, 0)], [("Sbf", h)])
                ph.op("pool", lambda e: e.tensor_copy(out=Sbf[h][:, 1, :], in_=S32[h][:, 1, :]),
                      [("S32", h, 1), ("Sbf", h)], [("Sbf", h)])

            def epilogue(h, t0, N):
                nch = N // 128
                ph.dma("sp", o0s[:, :, 0:N], o0_d[h * 512:(h + 1) * 512, t0:t0 + N].rearrange("(m p) t -> p m t", p=128),
                       writes=["o0s"], key="o0s")
                ph.dma("sp", rg[:, :, 0:N],
                       projB_d[2048 + h * 512:2048 + (h + 1) * 512, t0:t0 + N].rearrange("(m p) t -> p m t", p=128),
                       writes=["rg"], key="rg")
                ph.op("dve", lambda e: e.tensor_tensor(out=osb[h][:, :, 0:N], in0=osb[h][:, :, 0:N], in1=o0s[:, :, 0:N],
                                                       op=ALU.add), [("osb", h), "o0s"], [("osb", h)])
                ph.op("act", lambda e: e.activation(out=sqb[:, :, 0:N], in_=osb[h][:, :, 0:N], func=AF.Square),
                      [("osb", h)], ["sqb"])
                pa_ = nxt("patt", 2)

                def mm(e):
                    for m in range(4):
                        rr = e.matmul(patt[pa_][:, 0:N], lhsT=onesbf[:], rhs=sqb[:, m, 0:N], start=(m == 0), stop=(m == 3))
                    return rr
                ph.op("pe", mm, ["sqb"], [("patt", pa_)])
                ph.op("act", lambda e: e.activation(out=rstd[:, 0:N], in_=patt[pa_][:, 0:N], func=AF.Sqrt,
                                                    scale=1.0 / 512, bias=epsc[:, 0:1]), [("patt", pa_)], ["rstd"])
                ph.op("dve", lambda e: e.reciprocal(out=rstd[:, 0:N], in_=rstd[:, 0:N]), ["rstd"], ["rstd"])
                ph.op("act", lambda e: e.activation(out=rg[:, :, 0:N], in_=rg[:, :, 0:N], func=AF.Silu), ["rg"], ["rg"])
                for m in range(4):
                    ph.op("dve", lambda e, m=m: e.tensor_tensor(out=tm1[:, 0:N], in0=osb[h][:, m, 0:N], in1=rstd[:, 0:N],
                                                                op=ALU.mult), [("osb", h), "rstd"], ["tm1"])
                    ph.op("act", lambda e, m=m: e.activation(out=tm1[:, 0:N], in_=tm1[:, 0:N], func=AF.Identity,
                                                             scale=odg[:, m:m + 1]), ["tm1", "odg"], ["tm1"])
                    ph.op("dve", lambda e, m=m: e.tensor_tensor(out=ystg[:, m, 0:N], in0=tm1[:, 0:N], in1=rg[:, m, 0:N],
                                                                op=ALU.mult), ["tm1", "rg"], ["ystg"])
                ph.dma("pool", yT_d[h * 512:(h + 1) * 512, t0:t0 + N].rearrange("(m p) t -> p m t", p=128),
                       ystg[:, :, 0:N], reads=["ystg"], key="ystg")

            for d in range(2):
                for h in range(4):
                    ph.op("dve", lambda e, h=h: e.memset(S32[h][:], 0.0), [], [("S32", h, 0), ("S32", h, 1)])
                    ph.op("pool", lambda e, h=h: e.memset(Sbf[h][:], 0.0), [], [("Sbf", h)])
                    ph.dma("sp", ebs[h][:], ebend_d[d, 2 * h:2 * h + 2].rearrange("k p c -> p k c"),
                           writes=[("ebs", h)], key=("ebs", h))
                lat = [b for b in cfg.blocks if not b[2]]
                order = [cfg.blocks[0]] + (lat if d == 0 else lat[::-1])
                for (t0, N, is_ctx) in order:
                    nch = N // 128
                    for h in range(4):
                        ph.dma("sp", qes[h][:, :, 0:N],
                               qe_d[d, h * 256:(h + 1) * 256, t0:t0 + N].rearrange("(k p) t -> p k t", p=128),
                               writes=[("qes", h)], key=("qes", h))
                        ph.dma("sp", kes[h][:, :, 0:N],
                               ke_d[d, h * 256:(h + 1) * 256, t0:t0 + N].rearrange("(k p) t -> p k t", p=128),
                               writes=[("kes", h)], key=("kes", h))
                        ph.dma("sp", kns[h][:, 0:nch, :],
                               kend_d[d, h][t0:t0 + N, :].rearrange("(c s) k -> s c k", s=128),
                               writes=[("kns", h)], key=("kns", h))
                        ph.dma("sp", vs[h][:, 0:nch, :],
                               vTok_d[t0:t0 + N, h * 512:(h + 1) * 512].rearrange("(c s) v -> s c v", s=128),
                               writes=[("vs", h)], key=("vs", h))
                    lcs = list(range(nch)) if d == 0 else list(range(nch))[::-1]
                    for lc in lcs:
                        for h in range(4):
                            chunk_step(d, h, t0 // 128 + lc, lc)
                    if is_ctx and not need_ctx:
                        continue
                    for h in range(4):
                        if d == 0:
                            ph.dma("pool", o0_d[h * 512:(h + 1) * 512, t0:t0 + N].rearrange("(m p) t -> p m t", p=128),
                                   osb[h][:, :, 0:N], reads=[("osb", h)], key=("osb", h))
                        else:
                            epilogue(h, t0, N)
        stats.append(("gr", ph.stats))

    plan = [lambda: token_pass(None, 0)]
    for li in range(DEPTH):
        if li % 2 == 0:
            plan.append(lambda li=li: attention(li))
        else:
            plan.append(lambda li=li: gla_prep(li))
            plan.append(lambda li=li: gla_rec(li))
        plan.append(lambda li=li: token_pass(li, li + 1 if li + 1 < DEPTH else None))
    stop = getattr(cfg, "stop", None)
    for k, th in enumerate(plan):
        if stop is not None and k >= stop:
            break
        th()
    if dbg_d:
        with Phase(nc, "dbg") as ph:
            for nm, (dst, src) in dbg_d.items():
                ph.dma("sp", dst.ap(), src.ap(), key=("dbg", nm))
    return nc, stats


def _pc(v):
    v = np.asarray(v, np.float32)
    n = v.shape[-1] // 128
    w = v.reshape(v.shape[:-1] + (n, 128))
    return np.ascontiguousarray(np.moveaxis(w, -1, 0))


def _consts(L):
    T = CTX + L
    ident = np.eye(128, dtype=np.float32)
    R = np.zeros((128, 128), np.float32)
    for base in (0, 64):
        for g in (0, 32):
            for i in range(16):
                R[base + g + i, base + g + 16 + i] = -1.0
                R[base + g + 16 + i, base + g + i] = 1.0
    RT = np.ascontiguousarray(R.T)
    inv = (10000.0 ** (-np.arange(16, dtype=np.float32) / 16)).astype(np.float32)
    rows = L // GRID_W
    r = np.repeat(np.arange(rows, dtype=np.float32), GRID_W)
    col = np.tile(np.arange(GRID_W, dtype=np.float32), rows)
    ang_r = (r[:, None] * inv).astype(np.float32)
    ang_c = (col[:, None] * inv).astype(np.float32)
    ang64 = np.concatenate([ang_r, ang_r, ang_c, ang_c], axis=1)
    cos = np.cos(ang64).T.astype(np.float32)
    sin = np.sin(ang64).T.astype(np.float32)
    cosf = np.ones((128, T), np.float32)
    sinf = np.zeros((128, T), np.float32)
    cosf[0:64, CTX:] = cos
    cosf[64:128, CTX:] = cos
    sinf[0:64, CTX:] = sin
    sinf[64:128, CTX:] = sin
    rope = np.stack([cosf * 0.125, sinf * 0.125, cosf, sinf]).astype(np.float32)
    s = np.arange(128)
    tri = np.stack([(s[:, None] <= s[None, :]), (s[:, None] >= s[None, :])]).astype(np.float32)
    return dict(ident=ident, RT=RT, rope=rope, tri=tri)


def make_in_maps(cfg, inp):
    depth, NE, NO = cfg.depth, cfg.NE, cfg.NO
    B = inp["x"].shape[0]
    f = lambda a: np.ascontiguousarray(np.asarray(a, np.float32))
    common = dict(
        ada_w=f(inp["ada_w"]), ada_b_l=_pc(inp["ada_b"]),
        gains=_pc(np.concatenate([np.stack([inp["norm1_g"][i], inp["norm2_g"][i]]) for i in range(depth)]
                                 + [np.asarray(inp["final_g"])[None]], axis=0)),
        mlp_w1=f(inp["mlp_w1"]), mlp_w2=f(inp["mlp_w2"]),
        ev_w_in=f(inp["ev_w_in"]), ev_w_out=f(inp["ev_w_out"]),
        lamv=f(np.stack([inp["ev_lambda_q1"], inp["ev_lambda_k1"], inp["ev_lambda_q2"], inp["ev_lambda_k2"]],
                        axis=1))[None],
        subln=np.ascontiguousarray(f(inp["ev_subln_g"]).T),
        convw=_pc(np.concatenate([f(inp["ev_conv_w"]), f(inp["ev_conv_b"])[:, None, :]], axis=1)).transpose(0, 1, 3, 2).copy(),
        lru_wa=f(inp["ev_lru_wa"]), lru_wx=f(inp["ev_lru_wx"]),
        lru_v=_pc(np.stack([f(inp["ev_lru_ba"]), f(inp["ev_lru_bx"]), f(inp["ev_lru_lam"])], axis=2)
                  ).transpose(0, 1, 2, 4, 3).copy(),
    )
    if NO > 0:
        common.update(od_w_in=f(inp["od_w_in"]), od_w_out=f(inp["od_w_out"]), gate_w2=f(inp["od_gate_w2"]),
                      gate_b_l=_pc(inp["od_gate_b"]), od_norm_g_l=_pc(inp["od_norm_g"]))
    else:
        common.update(od_w_in=np.zeros((1, D, OD_IN), np.float32), od_w_out=np.zeros((1, D, D), np.float32),
                      gate_w2=np.zeros((1, 2, 16, 1024), np.float32), gate_b_l=np.zeros((128, 1, 2, 8), np.float32),
                      od_norm_g_l=np.zeros((128, 1, 4), np.float32))
    common.update(_consts(cfg.L))
    maps = []
    for b in range(B):
        m = dict(common)
        m["x"] = f(inp["x"][b])
        m["ctx"] = f(inp["ctx"][b])
        m["svec"] = np.ascontiguousarray(np.stack([_pc(inp["c"][b]), _pc(inp["c_ctx"])], axis=-1))
        maps.append(m)
    return maps


def run(cfg, inp, dbg=None):
    nc, stats = build_program(cfg, dbg=dbg)
    maps = make_in_maps(cfg, inp)
    res = run_bass_kernel_spmd(nc, maps, core_ids=list(range(len(maps))))
    return res, stats


ACTIVE = (0, 1, 4, 5)


def kernel(**inputs):
    cfg = Cfg(4096, 4)
    nc, _ = build_program(cfg)
    maps = make_in_maps(cfg, inputs)
    zero = dict(maps[0])
    zero["x"] = np.zeros_like(maps[0]["x"])
    zero["ctx"] = np.zeros_like(maps[0]["ctx"])
    zero["svec"] = np.zeros_like(maps[0]["svec"])
    full = [zero] * 8
    for b, c in enumerate(ACTIVE):
        full[c] = maps[b]
    res = run_bass_kernel_spmd(nc, full, core_ids=list(range(8)))
    return np.stack([np.asarray(res.results[c]["out"], np.float32) for c in ACTIVE], axis=0)
```

```python
import math
from collections import defaultdict

import numpy as np
import concourse.bass as bass
import concourse.mybir as mybir
from concourse.bass_utils import run_bass_kernel_spmd

F32 = mybir.dt.float32
BF16 = mybir.dt.bfloat16
AF = mybir.ActivationFunctionType
ALU = mybir.AluOpType
AX = mybir.AxisListType


class Prog:
    COMPUTE = ("pe", "act", "dve", "pool")
    SEM_EPOCH = 20000
    _uid = [0]

    def __init__(self, nc):
        self.nc = nc
        self.ops = []
        self.n_dma_sems = 0

    def op(self, eng, fn, reads=(), writes=(), dma_key=None):
        self.ops.append(dict(eng=eng, fn=fn, reads=tuple(reads), writes=tuple(writes),
                             dma_key=dma_key))

    def dma(self, q, out, in_, reads=(), writes=(), key=None, **kw):
        assert key is not None
        self.op(q, lambda e: e.dma_start(out=out, in_=in_, **kw), reads, writes, dma_key=key)

    def emit(self):
        nc = self.nc
        ops = self.ops
        n = len(ops)
        last_writer = {}
        readers = defaultdict(list)
        deps = [None] * n
        for i, o in enumerate(ops):
            d = set()
            for r in o["reads"]:
                if r in last_writer:
                    d.add(last_writer[r])
            for w in o["writes"]:
                if w in last_writer:
                    d.add(last_writer[w])
                for rd in readers[w]:
                    d.add(rd)
            d.discard(i)
            if o["dma_key"] is not None:
                pk = ("__dmakey__", o["dma_key"])
                if pk in last_writer:
                    d.add(last_writer[pk])
                last_writer[pk] = i
            if o["eng"] == "pe" and o["dma_key"] is None:
                d = {j for j in d if not (ops[j]["eng"] == "pe" and ops[j]["dma_key"] is None)}
            deps[i] = d
            for w in o["writes"]:
                last_writer[w] = i
                readers[w] = []
            for r in o["reads"]:
                if r not in o["writes"]:
                    readers[r].append(i)
        needed = [False] * n
        for d in deps:
            for j in d:
                needed[j] = True
        marker = [None] * n
        eng_cnt = defaultdict(int)
        eng_sems = defaultdict(list)
        dma_sem = {}
        dma_cnt = defaultdict(int)
        for i, o in enumerate(ops):
            if o["dma_key"] is not None:
                k = o["dma_key"]
                if k not in dma_sem:
                    Prog._uid[0] += 1
                    dma_sem[k] = nc.alloc_semaphore("dq%d" % Prog._uid[0])
                dma_cnt[k] += 16
                marker[i] = (dma_sem[k], dma_cnt[k], 16)
            elif needed[i]:
                e = o["eng"]
                c = eng_cnt[e]
                ep = c // self.SEM_EPOCH
                if ep >= len(eng_sems[e]):
                    Prog._uid[0] += 1
                    eng_sems[e].append(nc.alloc_semaphore("s_%s_%d" % (e, Prog._uid[0])))
                eng_cnt[e] = c + 1
                marker[i] = (eng_sems[e][ep], c % self.SEM_EPOCH + 1, 1)
        self.n_dma_sems = len(dma_sem)
        final_dma = [(dma_sem[k], dma_cnt[k]) for k in dma_sem]
        by_eng = defaultdict(list)
        for i, o in enumerate(ops):
            by_eng[o["eng"]].append(i)

        def run(engname, eng):
            seen = {}
            for i in by_eng.get(engname, []):
                o = ops[i]
                for j in sorted(deps[i]):
                    sem, val, _ = marker[j]
                    sid = id(sem)
                    if seen.get(sid, 0) < val:
                        eng.wait_ge(sem, val)
                        seen[sid] = val
                inst = o["fn"](eng)
                if marker[i] is not None:
                    sem, val, inc = marker[i]
                    inst.then_inc(sem, inc)
            if engname == "sp":
                for sem, val in final_dma:
                    eng.wait_ge(sem, val)

        with nc.Block() as block:
            @block.sync
            def _(e):
                run("sp", e)

            @block.tensor
            def _(e):
                run("pe", e)

            @block.scalar
            def _(e):
                run("act", e)

            @block.vector
            def _(e):
                run("dve", e)

            @block.gpsimd
            def _(e):
                run("pool", e)
        return dict(n_ops=n, n_dma_sems=len(dma_sem),
                    per_eng={k: len(v) for k, v in by_eng.items()})


D = 2048
DFF = 8192
CTX = 256
NMOD = 6
EPS = 1e-6
EV_IN = 5120
OD_IN = 6176
GRID_W = 64


class Cfg:
    def __init__(self, L, depth):
        self.L = L
        self.depth = depth
        self.T = CTX + L
        self.NE = (depth + 1) // 2
        self.NO = depth // 2
        self.blocks = [(0, CTX, True)] + [(CTX + 512 * j, 512, False) for j in range(L // 512)]
        self.NKT = self.T // 128


class Phase:
    _ctr = [0]

    def __init__(self, nc, name):
        self.nc = nc
        Phase._ctr[0] += 1
        self.name = "%s%d" % (name, Phase._ctr[0])
        self.stats = None

    def __enter__(self):
        self.cm = self.nc.cleanup_on_exit()
        self.cm.__enter__()
        self.p = Prog(self.nc)
        return self

    def __exit__(self, et, ev, tb):
        if et is not None:
            return False
        self.stats = self.p.emit()
        self.nc.all_engine_barrier()
        self.cm.__exit__(None, None, None)
        return False

    def sb(self, name, shape, dt):
        return self.nc.alloc_sbuf_tensor("%s_%s" % (self.name, name), list(shape), dt)

    def ps(self, name, shape=(128, 512), dt=F32):
        return self.nc.alloc_psum_tensor("%s_%s" % (self.name, name), list(shape), dt)

    def op(self, *a, **k):
        self.p.op(*a, **k)

    def dma(self, *a, **k):
        self.p.dma(*a, **k)


def wtile_view(wd, idx, kct, cw):
    return wd[idx].rearrange("p (k c) -> p k c", c=cw) if False else wd[idx]


def convert_list(src, dst, K, ncols, kct, cw):
    out = []
    nkg = K // (kct * 128)
    nblk = (ncols + cw - 1) // cw
    for kg in range(nkg):
        for blk in range(nblk):
            c0 = blk * cw
            w = min(cw, ncols - c0)
            s = src[kg * kct * 128:(kg + 1) * kct * 128, c0:c0 + w].rearrange("(k p) j -> p k j", p=128)
            d = dst[kg * nblk + blk][:, :, 0:w]
            out.append((d, s))
    return out


def build_program(cfg, dbg=None):
    nc = bass.Bass("TRN2", target_bir_lowering=False)
    L, T, DEPTH, NE, NO = cfg.L, cfg.T, cfg.depth, cfg.NE, cfg.NO
    NKT = cfg.NKT

    def din(name, shape, dt=F32):
        return nc.dram_tensor(name, list(shape), dt, kind="ExternalInput")

    x_d = din("x", [L, D])
    ctx_d = din("ctx", [CTX, D])
    svec_d = din("svec", [128, 16, 2])
    adaw_d = din("ada_w", [DEPTH, D, NMOD * D])
    adab_d = din("ada_b_l", [128, DEPTH, 96])
    gains_d = din("gains", [128, 2 * DEPTH + 1, 16])
    w1_d = din("mlp_w1", [DEPTH, D, DFF])
    w2_d = din("mlp_w2", [DEPTH, DFF, D])
    evin_d = din("ev_w_in", [NE, D, EV_IN])
    evout_d = din("ev_w_out", [NE, D, D])
    odin_d = din("od_w_in", [max(NO, 1), D, OD_IN])
    odout_d = din("od_w_out", [max(NO, 1), D, D])
    lamv_d = din("lamv", [1, NE, 4, 64])
    subln_d = din("subln", [128, NE])
    convw_d = din("convw", [128, NE, 8, 5])
    lruwa_d = din("lru_wa", [NE, 2, 8, 128, 128])
    lruwx_d = din("lru_wx", [NE, 2, 8, 128, 128])
    lruv_d = din("lru_v", [128, NE, 2, 8, 3])
    gw2_d = din("gate_w2", [max(NO, 1), 2, 16, 1024])
    gb_d = din("gate_b_l", [128, max(NO, 1), 2, 8])
    odg_d = din("od_norm_g_l", [128, max(NO, 1), 4])
    ident_d = din("ident", [128, 128])
    rt_d = din("RT", [128, 128])
    rope_d = din("rope", [4, 128, T])
    tri_d = din("tri", [2, 128, 128])
    out_d = nc.dram_tensor("out", [L, D], F32, kind="ExternalOutput")

    hT_d = nc.dram_tensor("hT", [D, T], F32)
    yT_d = nc.dram_tensor("yT", [D, T], BF16)
    projA_d = nc.dram_tensor("projA", [2048, T], BF16)
    projB_d = nc.dram_tensor("projB", [4128, T], F32)
    vTok_d = nc.dram_tensor("vTok", [T, 2048], BF16)
    wb_in = [nc.dram_tensor("wb_in%d" % i, [13 if i % 2 else 10, 128, 16, 512], BF16) for i in range(DEPTH)]
    wb_out = [nc.dram_tensor("wb_out%d" % i, [4, 128, 16, 512], BF16) for i in range(DEPTH)]
    wb_1 = [nc.dram_tensor("wb_1_%d" % i, [16, 128, 16, 512], BF16) for i in range(DEPTH)]
    wb_2 = [nc.dram_tensor("wb_2_%d" % i, [32, 128, 32, 128], BF16) for i in range(DEPTH)]
    dbg_d = {}
    if dbg:
        for nm in dbg:
            src = dict(hT=hT_d, yT=yT_d, projA=projA_d, projB=projB_d, vTok=vTok_d)[nm]
            dbg_d[nm] = (nc.dram_tensor("dbg_" + nm, list(src.shape), src.dtype, kind="ExternalOutput"), src)

    ident32 = nc.alloc_sbuf_tensor("ident32", [128, 128], F32)
    identbf = nc.alloc_sbuf_tensor("identbf", [128, 128], BF16)
    ones32 = nc.alloc_sbuf_tensor("ones32", [128, 128], F32)
    onesbf = nc.alloc_sbuf_tensor("onesbf", [128, 128], BF16)
    rt32 = nc.alloc_sbuf_tensor("rt32", [128, 128], F32)
    modS = nc.alloc_sbuf_tensor("modS", [128, DEPTH, 96, 2], F32)
    gains = nc.alloc_sbuf_tensor("gains_sb", [128, 2 * DEPTH + 1, 16], F32)
    zero1 = nc.alloc_sbuf_tensor("zero1", [128, 1], F32)
    epsc = nc.alloc_sbuf_tensor("epsc", [128, 1], F32)

    stats = []

    conv = []
    for i in range(DEPTH):
        j = i // 2
        if i % 2 == 0:
            lst = convert_list(evin_d[j], wb_in[i], D, EV_IN, 16, 512) + convert_list(evout_d[j], wb_out[i], D, D, 16, 512)
        else:
            lst = convert_list(odin_d[j], wb_in[i], D, OD_IN, 16, 512) + convert_list(odout_d[j], wb_out[i], D, D, 16, 512)
        lst += convert_list(w1_d[i], wb_1[i], D, DFF, 16, 512) + convert_list(w2_d[i], wb_2[i], DFF, D, 32, 128)
        conv.append(lst)
    conv_pos = [0] * DEPTH
    conv_ctr = [0]

    def emit_conv(ph, i, n, first=False):
        if i >= DEPTH:
            return
        while n > 0 and conv_pos[i] < len(conv[i]):
            d, s_ = conv[i][conv_pos[i]]
            conv_pos[i] += 1
            conv_ctr[0] += 1
            ph.dma("pool", d, s_, key=("cv", conv_ctr[0] % 8))
            n -= 1

    with Phase(nc, "cv") as ph:
        ph.dma("sp", ident32[:], ident_d.ap(), writes=["i32"], key="i32")
        ph.dma("sp", rt32[:], rt_d.ap(), writes=["rt"], key="rt")
        ph.dma("sp", gains[:], gains_d.ap(), writes=["gains"], key="gains")
        ph.op("dve", lambda e: e.tensor_copy(out=identbf[:], in_=ident32[:]), ["i32"], ["ibf"])
        ph.op("dve", lambda e: e.memset(ones32[:], 1.0), [], ["o32"])
        ph.op("dve", lambda e: e.memset(onesbf[:], 1.0), [], ["obf"])
        ph.op("dve", lambda e: e.memset(zero1[:], 0.0), [], ["z1"])
        ph.op("dve", lambda e: e.memset(epsc[:], EPS), [], ["epsc"])
        emit_conv(ph, 0, 10, first=True)
    stats.append(("cv", ph.stats))

    with Phase(nc, "mod") as ph:
        s32 = ph.sb("s32", [128, 16, 2], F32)
        sact = ph.sb("sact", [128, 16, 2], F32)
        adab = ph.sb("adab", [128, DEPTH, 96], F32)
        wts = [ph.sb("w%d" % k, [128, 16, 512], F32) for k in range(2)]
        rows = [ph.sb("row%d" % k, [2, 512], F32) for k in range(2)]
        prow = [ph.ps("prow%d" % k) for k in range(2)]
        pT = [ph.ps("pT%d" % k, [128, 192]) for k in range(2)]
        ph.dma("sp", s32[:], svec_d.ap(), writes=["s32"], key="s32")
        ph.dma("sp", adab[:], adab_d.ap(), writes=["adab"], key="adab")
        ph.op("act", lambda e: e.activation(out=sact[:], in_=s32[:], func=AF.Silu), ["s32"], ["sact"])
        n = 0
        for i in range(DEPTH):
            pTi = pT[i % 2]
            for cb in range(24):
                sl = n % 2
                n += 1
                src = adaw_d[i][:, cb * 512:(cb + 1) * 512].rearrange("(k p) j -> p k j", p=128)
                ph.dma("sp", wts[sl][:], src, writes=[("w", sl)], key=("w", sl))

                def mm(e, sl=sl):
                    for kc in range(16):
                        r = e.matmul(prow[sl][0:2, :], lhsT=sact[:, kc, :], rhs=wts[sl][:, kc, :],
                                     start=(kc == 0), stop=(kc == 15))
                    return r
                ph.op("pe", mm, ["sact", ("w", sl)], [("prow", sl)])
                ph.op("act", lambda e, sl=sl: e.activation(out=rows[sl][:], in_=prow[sl][0:2, :], func=AF.Copy),
                      [("prow", sl)], [("row", sl)])

                def tr(e, sl=sl, cb=cb, pTi=pTi):
                    for q in range(4):
                        c = cb * 4 + q
                        r = e.transpose(pTi[:, 2 * c:2 * c + 2], rows[sl][0:2, q * 128:(q + 1) * 128],
                                        ident32[0:2, 0:2])
                    return r
                ph.op("pe", tr, [("row", sl)], [("pT", i % 2)])
                emit_conv(ph, 0, 1)
            mv = modS[:, i, :, :]
            ph.op("dve", lambda e, pTi=pTi, mv=mv, i=i: e.tensor_tensor(
                out=mv, in0=pTi[:, :].rearrange("p (c r) -> p c r", r=2),
                in1=adab[:, i, :].unsqueeze(2).broadcast_to([128, 96, 2]), op=ALU.add),
                [("pT", i % 2), "adab"], [("modS", i)])
            for (j, g) in ((1, 2 * i), (4, 2 * i + 1)):
                sv = modS[:, i, j * 16:(j + 1) * 16, :]
                ph.op("dve", lambda e, sv=sv: e.tensor_scalar_add(out=sv, in0=sv, scalar1=1.0),
                      [("modS", i)], [("modS", i)])
                ph.op("dve", lambda e, sv=sv, g=g: e.tensor_tensor(
                    out=sv, in0=sv, in1=gains[:, g, :].unsqueeze(2).broadcast_to([128, 16, 2]), op=ALU.mult),
                    [("modS", i)], [("modS", i)])
        emit_conv(ph, 0, 10 ** 6)
    stats.append(("mod", ph.stats))

    def modcol(i, j, c, r):
        return modS[:, i, j * 16 + c, r:r + 1]

    def token_pass(li_prev, li_next):
        first = li_prev is None
        last = li_next is None
        with Phase(nc, "tp") as ph:
            hblk = ph.sb("hblk", [128, 16, 512], F32)
            ybuf = ph.sb("ybuf", [128, 16, 512], BF16)
            uT = ph.sb("uT", [128, 16, 512], BF16)
            hid = ph.sb("hid", [128, 32, 512], BF16)
            wt = [ph.sb("wt%d" % k, [128, 8192], BF16) for k in range(3)]
            rstd = ph.sb("rstd", [128, 512], F32)
            tmpc = [ph.sb("tmpc%d" % k, [128, 512], F32) for k in range(2)]
            sqf = [ph.sb("sqf%d" % k, [128, 512], F32) for k in range(2)]
            q32 = [ph.sb("q32_%d" % k, [128, 512], F32) for k in range(2)]
            tab = ph.sb("tab", [128, 4, 512], F32)
            t1 = [ph.sb("t1_%d" % k, [128, 512], F32) for k in range(2)]
            t2 = [ph.sb("t2_%d" % k, [128, 512], F32) for k in range(2)]
            stgA = [ph.sb("stgA%d" % k, [128, 512], BF16) for k in range(2)]
            stgB = [ph.sb("stgB%d" % k, [128, 512], F32) for k in range(2)]
            xtok = [ph.sb("xtok%d" % k, [128, 2048], F32) for k in range(2)]
            pl = [ph.ps("pl%d" % k) for k in range(3)]
            pstat = ph.ps("pstat")
            prope = ph.ps("prope")
            ptm = [ph.ps("ptm%d" % k) for k in range(2)]
            ptr = ph.ps("ptr")
            cnt = defaultdict(int)

            def nxt(name, mod):
                v = cnt[name] % mod
                cnt[name] += 1
                return v

            def load_w(wd_tile, nelem):
                sl = nxt("wt", 3)
                ph.dma("sp", wt[sl][:, 0:nelem], wd_tile.rearrange("p k c -> p (k c)"),
                       writes=[("wt", sl)], key=("wt", sl))
                return sl

            def norm(N, gcol, shcol, out_fn):
                ph.op("act", lambda e: e.activation(out=ybuf[:, :, 0:N], in_=hblk[:, :, 0:N], func=AF.Square),
                      ["hblk"], ["ybuf"])

                def mm(e):
                    for c in range(16):
                        r = e.matmul(pstat[:, 0:N], lhsT=onesbf[:], rhs=ybuf[:, c, 0:N],
                                     start=(c == 0), stop=(c == 15))
                    return r
                ph.op("pe", mm, ["ybuf"], ["pstat"])
                ph.op("act", lambda e: e.activation(out=rstd[:, 0:N], in_=pstat[:, 0:N], func=AF.Sqrt,
                                                    scale=1.0 / D, bias=epsc[:, 0:1]), ["pstat"], ["rstd"])
                ph.op("dve", lambda e: e.reciprocal(out=rstd[:, 0:N], in_=rstd[:, 0:N]), ["rstd"], ["rstd"])
                for c in range(16):
                    sl = nxt("tmpc", 2)
                    ph.op("dve", lambda e, c=c, sl=sl: e.tensor_tensor(
                        out=tmpc[sl][:, 0:N], in0=hblk[:, c, 0:N], in1=rstd[:, 0:N], op=ALU.mult),
                        ["hblk", "rstd"], [("tmpc", sl)])
                    out_fn(c, sl, gcol(c), shcol(c))

            def norm_to_uT(N, gcol, shcol):
                def out_fn(c, sl, g, sh):
                    ph.op("act", lambda e: e.activation(out=uT[:, c, 0:N], in_=tmpc[sl][:, 0:N],
                                                        func=AF.Identity, scale=g, bias=sh),
                          [("tmpc", sl)], ["uT"])
                norm(N, gcol, shcol, out_fn)

            def linear_fm(xT, xkey, wd, wblocks, N, epi, kct=16, cw=512, koff=0):
                for (ti, oc0, ncol) in wblocks:
                    sl = load_w(wd[ti], kct * cw)
                    wv = wt[sl][:, 0:kct * cw].rearrange("p (k c) -> p k c", c=cw)
                    for m in range((ncol + 127) // 128):
                        mw = min(128, ncol - m * 128)
                        b = nxt("pl", 3)

                        def mm(e, wv=wv, m=m, mw=mw, b=b):
                            for kc in range(kct):
                                r = e.matmul(pl[b][0:mw, 0:N], lhsT=wv[:, kc, m * 128:m * 128 + mw],
                                             rhs=xT[:, koff + kc, 0:N], start=(kc == 0), stop=(kc == kct - 1))
                            return r
                        ph.op("pe", mm, [xkey, ("wt", sl)], [("pl", b)])
                        epi(oc0 + m, b, mw)

            def linear_tm(wd, wblocks, N, epi):
                for (ti, cb0) in wblocks:
                    sl = load_w(wd[ti], 8192)
                    wv = wt[sl][:, :].rearrange("p (k c) -> p k c", c=512)
                    for tt in range(N // 128):
                        b = nxt("ptm", 2)

                        def mm(e, wv=wv, tt=tt, b=b):
                            for kc in range(16):
                                r = e.matmul(ptm[b][:, :], lhsT=uT[:, kc, tt * 128:(tt + 1) * 128],
                                             rhs=wv[:, kc, :], start=(kc == 0), stop=(kc == 15))
                            return r
                        ph.op("pe", mm, ["uT", ("wt", sl)], [("ptm", b)])
                        epi(cb0, tt, b)

            def residual_epi(N, gate_fn):
                def epi(oc, b, mw):
                    g = gate_fn(oc)
                    ph.op("dve", lambda e: e.scalar_tensor_tensor(
                        out=hblk[:, oc, 0:N], in0=pl[b][:, 0:N], scalar=g, in1=hblk[:, oc, 0:N],
                        op0=ALU.mult, op1=ALU.add), [("pl", b), "hblk"], ["hblk"])
                return epi

            def do_block(t0, N, is_ctx):
                r = 1 if is_ctx else 0
                if last and is_ctx:
                    return
                if first:
                    src = ctx_d if is_ctx else x_d
                    r0 = 0 if is_ctx else t0 - CTX
                    for tt in range(N // 128):
                        xs = nxt("xtok", 2)
                        ph.dma("sp", xtok[xs][:], src[r0 + tt * 128:r0 + (tt + 1) * 128, :],
                               writes=[("xtok", xs)], key=("xtok", xs))
                        for c0 in range(0, 16, 4):
                            def trp(e, xs=xs, c0=c0):
                                for q in range(4):
                                    rr = e.transpose(ptr[:, q * 128:(q + 1) * 128],
                                                     xtok[xs][:, (c0 + q) * 128:(c0 + q + 1) * 128], ident32[:])
                                return rr
                            ph.op("pe", trp, [("xtok", xs)], ["ptr"])
                            ph.op("act", lambda e, c0=c0, tt=tt: e.activation(
                                out=hblk[:, c0:c0 + 4, tt * 128:(tt + 1) * 128],
                                in_=ptr[:, :].rearrange("p (q t) -> p q t", t=128), func=AF.Copy),
                                ["ptr"], ["hblk"])
                else:
                    ph.dma("sp", hblk[:, :, 0:N], hT_d[:, t0:t0 + N].rearrange("(c p) t -> p c t", p=128),
                           writes=["hblk"], key="hblk")
                if not first:
                    i = li_prev
                    ph.dma("sp", ybuf[:, :, 0:N], yT_d[:, t0:t0 + N].rearrange("(c p) t -> p c t", p=128),
                           writes=["ybuf"], key="ybuf")
                    linear_fm(ybuf, "ybuf", wb_out[i], [(k, 4 * k, 512) for k in range(4)], N,
                              residual_epi(N, lambda oc: modcol(i, 2, oc, r)))
                    norm_to_uT(N, lambda c: modcol(i, 4, c, r), lambda c: modcol(i, 3, c, r))
                    for half in range(2):
                        def hid_epi(oc, b, mw, half=half):
                            j = oc - half * 32
                            s = nxt("sqf", 2)
                            ph.op("act", lambda e: e.activation(out=sqf[s][:, 0:N], in_=pl[b][:, 0:N],
                                                                func=AF.Square), [("pl", b)], [("sqf", s)])
                            ph.op("dve", lambda e: e.scalar_tensor_tensor(
                                out=hid[:, j, 0:N], in0=pl[b][:, 0:N], scalar=0.0, in1=sqf[s][:, 0:N],
                                op0=ALU.is_gt, op1=ALU.mult), [("pl", b), ("sqf", s)], ["hid"])
                        linear_fm(uT, "uT", wb_1[i], [(half * 8 + k, half * 32 + 4 * k, 512) for k in range(8)],
                                  N, hid_epi)
                        linear_fm(hid, "hid", wb_2[i], [(half * 16 + f, f, 128) for f in range(16)], N,
                                  residual_epi(N, lambda oc: modcol(i, 5, oc, r)), kct=32, cw=128)
                if not last:
                    ph.dma("pool", hT_d[:, t0:t0 + N].rearrange("(c p) t -> p c t", p=128), hblk[:, :, 0:N],
                           reads=["hblk"], writes=[], key="hst")
                if not last:
                    i = li_next
                    norm_to_uT(N, lambda c: modcol(i, 1, c, r), lambda c: modcol(i, 0, c, r))
                    if i % 2 == 0:
                        ph.dma("sp", tab[:, :, 0:N], rope_d[:, :, t0:t0 + N].rearrange("f p t -> p f t"),
                               writes=["tab"], key="tab")

                        def qk_epi(oc, b, mw):
                            isk = 1 if oc >= 8 else 0
                            s = nxt("q32", 2)
                            ph.op("act", lambda e: e.activation(out=q32[s][:, 0:N], in_=pl[b][:, 0:N],
                                                                func=AF.Copy), [("pl", b)], [("q32", s)])
                            ph.op("pe", lambda e: e.matmul(prope[:, 0:N], lhsT=rt32[:], rhs=q32[s][:, 0:N],
                                                           start=True, stop=True), [("q32", s)], ["prope"])
                            ph.op("dve", lambda e: e.tensor_tensor(out=t1[s][:, 0:N], in0=q32[s][:, 0:N],
                                                                   in1=tab[:, 2 * isk, 0:N], op=ALU.mult),
                                  [("q32", s), "tab"], [("t1", s)])
                            ph.op("dve", lambda e: e.tensor_tensor(out=t2[s][:, 0:N], in0=prope[:, 0:N],
                                                                   in1=tab[:, 2 * isk + 1, 0:N], op=ALU.mult),
                                  ["prope", "tab"], [("t2", s)])
                            ph.op("pool", lambda e: e.tensor_tensor(out=stgA[s][:, 0:N], in0=t1[s][:, 0:N],
                                                                    in1=t2[s][:, 0:N], op=ALU.add),
                                  [("t1", s), ("t2", s)], [("stgA", s)])
                            ph.dma("pool", projA_d[oc * 128:(oc + 1) * 128, t0:t0 + N], stgA[s][:, 0:N],
                                   reads=[("stgA", s)], key=("stgA", s))

                        def f32_epi_rows(row_of):
                            def epi(oc, b, mw):
                                s = nxt("stgB", 2)
                                ph.op("act", lambda e: e.activation(out=stgB[s][0:mw, 0:N], in_=pl[b][0:mw, 0:N],
                                                                    func=AF.Copy), [("pl", b)], [("stgB", s)])
                                r0 = row_of(oc)
                                ph.dma("pool", projB_d[r0:r0 + mw, t0:t0 + N], stgB[s][0:mw, 0:N],
                                       reads=[("stgB", s)], key=("stgB", s))
                            return epi

                        def v_epi(ncolblk):
                            def epi(cb0, tt, b):
                                s = nxt("stgA", 2)
                                ph.op("act", lambda e: e.activation(out=stgA[s][:, :], in_=ptm[b][:, :],
                                                                    func=AF.Copy), [("ptm", b)], [("stgA", s)])
                                ph.dma("pool", vTok_d[t0 + tt * 128:t0 + (tt + 1) * 128, cb0 * 512:(cb0 + 1) * 512],
                                       stgA[s][:, :], reads=[("stgA", s)], key=("stgA", s))
                            return epi
                        linear_fm(uT, "uT", wb_in[i], [(k, 4 * k, 512) for k in range(4)], N, qk_epi)
                        linear_tm(wb_in[i], [(4, 0), (5, 1)], N, v_epi(2))
                        linear_fm(uT, "uT", wb_in[i], [(k, 4 * k, 512) for k in range(6, 10)], N,
                                  f32_epi_rows(lambda oc: (oc - 24) * 128))
                    else:
                        def f32_epi_rows(row_of):
                            def epi(oc, b, mw):
                                s = nxt("stgB", 2)
                                ph.op("act", lambda e: e.activation(out=stgB[s][0:mw, 0:N], in_=pl[b][0:mw, 0:N],
                                                                    func=AF.Copy), [("pl", b)], [("stgB", s)])
                                r0 = row_of(oc)
                                ph.dma("pool", projB_d[r0:r0 + mw, t0:t0 + N], stgB[s][0:mw, 0:N],
                                       reads=[("stgB", s)], key=("stgB", s))
                            return epi

                        def v_epi(cb0, tt, b):
                            s = nxt("stgA", 2)
                            ph.op("act", lambda e: e.activation(out=stgA[s][:, :], in_=ptm[b][:, :],
                                                                func=AF.Copy), [("ptm", b)], [("stgA", s)])
                            ph.dma("pool", vTok_d[t0 + tt * 128:t0 + (tt + 1) * 128, cb0 * 512:(cb0 + 1) * 512],
                                   stgA[s][:, :], reads=[("stgA", s)], key=("stgA", s))
                        linear_fm(uT, "uT", wb_in[i], [(k, 4 * k, 512) for k in range(4)], N,
                                  f32_epi_rows(lambda oc: oc * 128))
                        linear_tm(wb_in[i], [(4 + k, k) for k in range(4)], N, v_epi)
                        linear_fm(uT, "uT", wb_in[i], [(k, 4 * k, 512) for k in range(8, 12)] + [(12, 48, 32)], N,
                                  f32_epi_rows(lambda oc: 2048 + (oc - 32) * 128))
                else:
                    fg = 2 * DEPTH

                    def out_fn(c, sl, g, sh):
                        ph.op("act", lambda e: e.activation(out=hblk[:, c, 0:N], in_=tmpc[sl][:, 0:N],
                                                            func=AF.Identity, scale=g, bias=sh),
                              [("tmpc", sl), "hblk"], ["hblk"])
                    norm(N, lambda c: gains[:, fg, c:c + 1], lambda c: zero1[:, 0:1], out_fn)
                    for tt in range(N // 128):
                        xs = nxt("xtok", 2)
                        for c0 in range(0, 16, 4):
                            def trp(e, c0=c0, tt=tt):
                                for q in range(4):
                                    rr = e.transpose(ptr[:, q * 128:(q + 1) * 128],
                                                     hblk[:, c0 + q, tt * 128:(tt + 1) * 128], ident32[:])
                                return rr
                            ph.op("pe", trp, ["hblk"], ["ptr"])
                            ph.op("act", lambda e, c0=c0, xs=xs: e.activation(
                                out=xtok[xs][:, c0 * 128:(c0 + 4) * 128], in_=ptr[:, :], func=AF.Copy),
                                ["ptr"], [("xtok", xs)])
                        r0 = t0 - CTX + tt * 128
                        ph.dma("pool", out_d[r0:r0 + 128, :], xtok[xs][:], reads=[("xtok", xs)], key=("xtok", xs))
            for blk in cfg.blocks:
                do_block(*blk)
        stats.append(("tp", ph.stats))

    def attention(li):
        j = li // 2
        lam_init = 0.8 - 0.6 * math.exp(-0.3 * li)
        need_ctx = li < DEPTH - 1
        with Phase(nc, "at") as ph:
            kT = [ph.sb("kT%d" % k, [128, T], BF16) for k in range(2)]
            vt = [ph.sb("vt%d" % k, [128, NKT, 128], BF16) for k in range(2)]
            qT = [ph.sb("qT%d" % k, [128, 512], BF16) for k in range(2)]
            r1 = ph.sb("r1", [128, 512], F32)
            r2 = ph.sb("r2", [128, 512], F32)
            oa = ph.sb("oa", [128, 512], F32)
            ob = ph.sb("ob", [128, 512], F32)
            osb = ph.sb("osb", [128, 512], F32)
            sq = ph.sb("sq", [128, 512], F32)
            rs = ph.sb("rs", [128, 512], F32)
            yv = ph.sb("yv", [128, 512], F32)
            ystg = [ph.sb("ystg%d" % k, [128, 512], BF16) for k in range(2)]
            lv = ph.sb("lv", [1, 4, 64], F32)
            prod = ph.sb("prod", [1, 2, 64], F32)
            ss2 = ph.sb("ss2", [1, 2], F32)
            e2 = ph.sb("e2", [1, 2], F32)
            lb = ph.sb("lb", [128, 2], F32)
            neglam = ph.sb("neglam", [128, 1], F32)
            sg = ph.sb("sg", [128, 1], F32)
            subl = ph.sb("subl", [128, NE], F32)
            NSB = 2
            sbk = [ph.ps("sbk%d" % k, [128, 1024]) for k in range(NSB)]
            o1, o2, d1, pst = ph.ps("o1"), ph.ps("o2"), ph.ps("d1"), ph.ps("pst")
            acc2 = ph.sb("acc2", [128, 512], F32)
            pp = [ph.sb("pp%d" % k, [128, 2, 512], BF16) for k in range(3)]
            cnt = defaultdict(int)

            def nxt(name, mod):
                v = cnt[name] % mod
                cnt[name] += 1
                return v

            ph.dma("sp", lv[:], lamv_d[0:1, j], writes=["lv"], key="lv")
            ph.dma("sp", subl[:], subln_d.ap(), writes=["subl"], key="subl")
            ph.op("dve", lambda e: e.tensor_tensor(out=prod[0:1, 0, :], in0=lv[0:1, 0, :], in1=lv[0:1, 1, :],
                                                   op=ALU.mult), ["lv"], ["prod"])
            ph.op("dve", lambda e: e.tensor_tensor(out=prod[0:1, 1, :], in0=lv[0:1, 2, :], in1=lv[0:1, 3, :],
                                                   op=ALU.mult), ["lv", "prod"], ["prod"])
            ph.op("dve", lambda e: e.reduce_sum(out=ss2[0:1, :], in_=prod[0:1, :, :], axis=AX.X), ["prod"], ["ss2"])
            ph.op("act", lambda e: e.activation(out=e2[0:1, :], in_=ss2[0:1, :], func=AF.Exp), ["ss2"], ["e2"])
            ph.op("pe", lambda e: e.matmul(pst[:, 0:2], lhsT=ones32[0:1, :], rhs=e2[0:1, 0:2], start=True, stop=True),
                  ["e2"], ["pst"])
            ph.op("act", lambda e: e.activation(out=lb[:], in_=pst[:, 0:2], func=AF.Copy), ["pst"], ["lb"])
            ph.op("dve", lambda e: e.tensor_tensor(out=neglam[:], in0=lb[:, 1:2], in1=lb[:, 0:1], op=ALU.subtract),
                  ["lb"], ["neglam"])
            ph.op("dve", lambda e: e.tensor_scalar_add(out=neglam[:], in0=neglam[:], scalar1=-lam_init),
                  ["neglam"], ["neglam"])
            ph.op("dve", lambda e: e.tensor_scalar_mul(out=sg[:], in0=subl[:, j:j + 1], scalar1=1.0 - lam_init),
                  ["subl"], ["sg"])

            pend = {"stages": None}

            def run_stage(k):
                st = pend["stages"]
                if st is not None and st[k] is not None:
                    f_ = st[k]
                    st[k] = None
                    f_()

            def flush():
                for k in range(3):
                    run_stage(k)

            def do_qblock(h, hs, t0, N, is_ctx):
                qs = nxt("qT", 2)
                ph.dma("sp", qT[qs][:, 0:N], projA_d[h * 128:(h + 1) * 128, t0:t0 + N],
                       writes=[("qT", qs)], key=("qT", qs))
                kts = [0, 1] if is_ctx else list(range(NKT))
                nk = len(kts)

                def QK(idx):
                    kt = kts[idx]
                    b = idx % NSB

                    def f(e):
                        e.matmul(sbk[b][:, 0:N], lhsT=kT[hs][0:64, kt * 128:(kt + 1) * 128], rhs=qT[qs][0:64, 0:N],
                                 start=True, stop=True)
                        return e.matmul(sbk[b][:, 512:512 + N], lhsT=kT[hs][64:128, kt * 128:(kt + 1) * 128],
                                        rhs=qT[qs][64:128, 0:N], start=True, stop=True)
                    ph.op("pe", f, [("kT", hs), ("qT", qs)], [("sbk", b)])

                def EXP(idx):
                    b = idx % NSB
                    s = idx % 3
                    ph.op("act", lambda e: e.activation(
                        out=pp[s][:, :, 0:N], in_=sbk[b][:, :].rearrange("p (m n) -> p m n", m=2)[:, :, 0:N],
                        func=AF.Exp), [("sbk", b)], [("pp", s)])

                def PV(idx):
                    kt = kts[idx]
                    s = idx % 3
                    st, sp_ = (idx == 0), (idx == nk - 1)

                    def f(e):
                        e.matmul(o1[:, 0:N], lhsT=vt[hs][:, kt, :], rhs=pp[s][:, 0, 0:N], start=st, stop=sp_)
                        e.matmul(d1[:, 0:N], lhsT=onesbf[:], rhs=pp[s][:, 0, 0:N], start=st, stop=sp_)
                        return e.matmul(o2[:, 0:N], lhsT=vt[hs][:, kt, :], rhs=pp[s][:, 1, 0:N], start=st, stop=sp_)
                    ph.op("pe", f, [("vt", hs), ("pp", s)], ["acc"])
                    if st:
                        ph.op("dve", lambda e: e.tensor_copy(out=acc2[:, 0:N], in_=pp[s][:, 1, 0:N]), [("pp", s)], ["acc2"])
                    else:
                        ph.op("dve", lambda e: e.tensor_tensor(out=acc2[:, 0:N], in0=pp[s][:, 1, 0:N], in1=acc2[:, 0:N],
                                                               op=ALU.add), [("pp", s), "acc2"], ["acc2"])

                QK(0)
                if nk > 1:
                    QK(1)
                for idx in range(nk):
                    EXP(idx)
                    if idx + 2 < nk:
                        QK(idx + 2)
                    PV(idx)
                    if idx == min(2, nk - 1):
                        run_stage(0)
                    if idx == min(6, nk - 1):
                        run_stage(1)
                    if idx == min(8, nk - 1):
                        run_stage(2)
                ph.op("pe", lambda e: e.matmul(pst[:, 0:N], lhsT=ones32[:], rhs=acc2[:, 0:N], start=True, stop=True),
                      ["acc2"], ["pst"])
                ph.op("dve", lambda e: e.reciprocal(out=r1[:, 0:N], in_=d1[:, 0:N]), ["acc"], ["r1"])
                ph.op("dve", lambda e: e.tensor_tensor(out=oa[:, 0:N], in0=o1[:, 0:N], in1=r1[:, 0:N], op=ALU.mult),
                      ["acc", "r1"], ["oa"])
                ph.op("dve", lambda e: e.reciprocal(out=r2[:, 0:N], in_=pst[:, 0:N]), ["pst"], ["r2"])
                ph.op("dve", lambda e: e.tensor_tensor(out=ob[:, 0:N], in0=o2[:, 0:N], in1=r2[:, 0:N], op=ALU.mult),
                      ["acc", "r2"], ["ob"])

                def stage0():
                    ph.op("dve", lambda e: e.scalar_tensor_tensor(out=osb[:, 0:N], in0=ob[:, 0:N], scalar=neglam[:, 0:1],
                                                                  in1=oa[:, 0:N], op0=ALU.mult, op1=ALU.add),
                          ["oa", "ob", "neglam"], ["osb"])
                    ph.op("dve", lambda e: e.tensor_tensor(out=sq[:, 0:N], in0=osb[:, 0:N], in1=osb[:, 0:N], op=ALU.mult),
                          ["osb"], ["sq"])
                    ph.op("pe", lambda e: e.matmul(pst[:, 0:N], lhsT=ones32[:], rhs=sq[:, 0:N], start=True, stop=True),
                          ["sq"], ["pst"])

                def stage1():
                    ph.op("act", lambda e: e.activation(out=rs[:, 0:N], in_=pst[:, 0:N], func=AF.Ln,
                                                        scale=1.0 / 128, bias=epsc[:, 0:1]), ["pst"], ["rs"])
                    ph.op("act", lambda e: e.activation(out=rs[:, 0:N], in_=rs[:, 0:N], func=AF.Exp, scale=-0.5),
                          ["rs"], ["rs"])

                def stage2():
                    ph.op("dve", lambda e: e.tensor_tensor(out=yv[:, 0:N], in0=osb[:, 0:N], in1=rs[:, 0:N], op=ALU.mult),
                          ["osb", "rs"], ["yv"])
                    ys = nxt("ystg", 2)
                    ph.op("dve", lambda e: e.tensor_scalar_mul(out=ystg[ys][:, 0:N], in0=yv[:, 0:N], scalar1=sg[:, 0:1]),
                          ["yv", "sg"], [("ystg", ys)])
                    ph.dma("pool", yT_d[h * 128:(h + 1) * 128, t0:t0 + N], ystg[ys][:, 0:N],
                           reads=[("ystg", ys)], key=("ystg", ys))
                flush()
                pend["stages"] = [stage0, stage1, stage2]

            for h in range(8):
                hs = h % 2
                ph.dma("sp", kT[hs][:], projA_d[1024 + h * 128:1024 + (h + 1) * 128, :],
                       writes=[("kT", hs)], key=("kT", hs))
                ph.dma("sp", vt[hs][:], vTok_d[:, h * 128:(h + 1) * 128].rearrange("(kt p) e -> p kt e", p=128),
                       writes=[("vt", hs)], key=("vt", hs))
                for (t0, N, is_ctx) in cfg.blocks:
                    if is_ctx and not need_ctx:
                        continue
                    do_qblock(h, hs, t0, N, is_ctx)
                    emit_conv(ph, li + 1, 1)
            flush()
            emit_conv(ph, li + 1, 10 ** 6)
        stats.append(("at", ph.stats))

    def rglru(li):
        j = li // 2
        with Phase(nc, "lr") as ph:
            B = [ph.sb("B%d" % k, [128, T], F32) for k in range(7)]
            ystg = ph.sb("ystg", [128, T], BF16)
            wa = ph.sb("wa", [128, 2, 8, 128], F32)
            wx = ph.sb("wx", [128, 2, 8, 128], F32)
            cw = ph.sb("cw", [128, 8, 5], F32)
            lv = ph.sb("lv", [128, 2, 8, 3], F32)
            negc = ph.sb("negc", [128, 2, 8], F32)
            etmp = ph.sb("etmp", [128, 2, 8], F32)
            pa = [ph.ps("pa%d" % k) for k in range(2)]
            px = [ph.ps("px%d" % k) for k in range(2)]
            for d in range(2):
                ph.dma("sp", wa[:, d, :, :], lruwa_d[j, d].rearrange("n i o -> i n o"), writes=["wa"], key=("wa", d))
                ph.dma("sp", wx[:, d, :, :], lruwx_d[j, d].rearrange("n i o -> i n o"), writes=["wx"], key=("wx", d))
            ph.dma("sp", cw[:], convw_d[:, j], writes=["cw"], key="cw")
            ph.dma("sp", lv[:], lruv_d[:, j], writes=["lv"], key="lv")
            ph.op("act", lambda e: e.activation(out=etmp[:], in_=lv[:, :, :, 2], func=AF.Exp, scale=-1.0),
                  ["lv"], ["etmp"])
            ph.op("act", lambda e: e.activation(out=etmp[:], in_=etmp[:], func=AF.Ln, bias=1.0), ["etmp"], ["etmp"])
            ph.op("dve", lambda e: e.tensor_scalar_mul(out=negc[:], in0=etmp[:], scalar1=-8.0), ["etmp"], ["negc"])
            segs = [(0, CTX), (CTX, T)]
            tb = [(t0, N) for (t0, N, _) in cfg.blocks]

            def chunk(n):
                rec, xx, gt, t3, gi, hs_, hh = B
                ph.dma("sp", rec[:], projB_d[1024 + n * 128:1024 + (n + 1) * 128, :], writes=["B0"], key="B0")
                ph.dma("sp", gt[:], projB_d[n * 128:(n + 1) * 128, :], writes=["B2"], key="B2")
                ph.op("act", lambda e: e.activation(out=xx[:], in_=rec[:], func=AF.Identity, scale=cw[:, n, 2:3],
                                                    bias=cw[:, n, 4:5]), ["B0", "cw"], ["B1"])
                for (a0, a1) in segs:
                    for (tap, sh) in ((0, -2), (1, -1), (3, 1)):
                        lo, hi = max(a0, a0 - sh), min(a1, a1 - sh)
                        ph.op("dve", lambda e, lo=lo, hi=hi, sh=sh, tap=tap: e.scalar_tensor_tensor(
                            out=xx[:, lo:hi], in0=rec[:, lo + sh:hi + sh], scalar=cw[:, n, tap:tap + 1],
                            in1=xx[:, lo:hi], op0=ALU.mult, op1=ALU.add), ["B0", "B1", "cw"], ["B1"])
                ph.op("dve", lambda e: e.tensor_tensor(out=t3[:], in0=gt[:], in1=gt[:], op=ALU.mult), ["B2"], ["B3"])
                ph.op("dve", lambda e: e.tensor_scalar(out=t3[:], in0=t3[:], scalar1=0.044715, scalar2=1.0,
                                                       op0=ALU.mult, op1=ALU.add), ["B3"], ["B3"])
                ph.op("dve", lambda e: e.tensor_tensor(out=t3[:], in0=t3[:], in1=gt[:], op=ALU.mult),
                      ["B3", "B2"], ["B3"])
                ph.op("act", lambda e: e.activation(out=t3[:], in_=t3[:], func=AF.Sigmoid,
                                                    scale=2.0 * math.sqrt(2.0 / math.pi)), ["B3"], ["B3"])
                ph.op("dve", lambda e: e.tensor_tensor(out=gt[:], in0=t3[:], in1=gt[:], op=ALU.mult),
                      ["B3", "B2"], ["B2"])
                for d in range(2):
                    for bi, (t0, N) in enumerate(tb):
                        pb = bi % 2
                        ph.op("pe", lambda e, t0=t0, N=N, pb=pb, d=d: e.matmul(
                            pa[pb][:, 0:N], lhsT=wa[:, d, n, :], rhs=xx[:, t0:t0 + N], start=True, stop=True),
                            ["wa", "B1"], [("pa", pb)])
                        ph.op("act", lambda e, t0=t0, N=N, pb=pb, d=d: e.activation(
                            out=rec[:, t0:t0 + N], in_=pa[pb][:, 0:N], func=AF.Sigmoid, bias=lv[:, d, n, 0:1]),
                            [("pa", pb), "lv"], ["B0"])
                        ph.op("pe", lambda e, t0=t0, N=N, pb=pb, d=d: e.matmul(
                            px[pb][:, 0:N], lhsT=wx[:, d, n, :], rhs=xx[:, t0:t0 + N], start=True, stop=True),
                            ["wx", "B1"], [("px", pb)])
                        ph.op("act", lambda e, t0=t0, N=N, pb=pb, d=d: e.activation(
                            out=gi[:, t0:t0 + N], in_=px[pb][:, 0:N], func=AF.Sigmoid, bias=lv[:, d, n, 1:2]),
                            [("px", pb), "lv"], ["B4"])
                    ph.op("act", lambda e, d=d: e.activation(out=rec[:], in_=rec[:], func=AF.Exp,
                                                             scale=negc[:, d, n:n + 1]), ["B0", "negc"], ["B0"])
                    ph.op("dve", lambda e: e.tensor_tensor(out=t3[:], in0=rec[:], in1=rec[:], op=ALU.mult),
                          ["B0"], ["B3"])
                    ph.op("act", lambda e: e.activation(out=t3[:], in_=t3[:], func=AF.Sqrt, scale=-1.0, bias=1.0),
                          ["B3"], ["B3"])
                    ph.op("dve", lambda e: e.tensor_tensor(out=gi[:], in0=gi[:], in1=xx[:], op=ALU.mult),
                          ["B4", "B1"], ["B4"])
                    ph.op("dve", lambda e: e.tensor_tensor(out=gi[:], in0=gi[:], in1=t3[:], op=ALU.mult),
                          ["B4", "B3"], ["B4"])
                    if d == 0:
                        ph.op("dve", lambda e: e.tensor_tensor_scan(out=hs_[:], data0=rec[:], data1=gi[:],
                                                                    initial=0.0, op0=ALU.mult, op1=ALU.add),
                              ["B0", "B4"], ["B5"])
                    else:
                        ph.op("dve", lambda e: e.tensor_tensor_scan(
                            out=hh[:, 0:CTX][:, ::-1], data0=rec[:, 0:CTX][:, ::-1], data1=gi[:, 0:CTX][:, ::-1],
                            initial=0.0, op0=ALU.mult, op1=ALU.add), ["B0", "B4"], ["B6"])
                        ph.op("dve", lambda e: e.tensor_tensor_scan(
                            out=hh[:, CTX:T][:, ::-1], data0=rec[:, CTX:T][:, ::-1], data1=gi[:, CTX:T][:, ::-1],
                            initial=hh[:, 0:1], op0=ALU.mult, op1=ALU.add), ["B0", "B4", "B6"], ["B6"])
                        ph.op("dve", lambda e: e.tensor_tensor(out=hs_[:], in0=hs_[:], in1=hh[:], op=ALU.add),
                              ["B5", "B6"], ["B5"])
                ph.op("dve", lambda e: e.tensor_tensor(out=ystg[:], in0=hs_[:], in1=gt[:], op=ALU.mult),
                      ["B5", "B2"], ["ystg"])
                ph.dma("pool", yT_d[1024 + n * 128:1024 + (n + 1) * 128, :], ystg[:], reads=["ystg"], key="ystg")

            for n in range(8):
                chunk(n)
        stats.append(("lr", ph.stats))

    NCH = T // 128
    qe_d = nc.dram_tensor("qe", [2, 1024, T], BF16)
    ke_d = nc.dram_tensor("ke", [2, 1024, T], BF16)
    kend_d = nc.dram_tensor("kend", [2, 4, T, 256], BF16)
    ebend_d = nc.dram_tensor("ebend", [2, 8, 128, NCH], F32)
    o0_d = nc.dram_tensor("o0", [2048, T], F32)

    def gla_prep(li):
        j = li // 2
        tb = [(t0, N) for (t0, N, _) in cfg.blocks]
        with Phase(nc, "gp") as ph:
            qv = ph.sb("qv", [128, T], F32)
            kv = ph.sb("kv", [128, T], F32)
            spb = ph.sb("spb", [128, T], F32)
            bcs = ph.sb("bcs", [128, T], F32)
            E = ph.sb("E", [128, T], F32)
            tmp = ph.sb("tmp", [128, T], F32)
            mask = ph.sb("mask", [128, T], F32)
            qeb = ph.sb("qeb", [128, T], BF16)
            keb = ph.sb("keb", [128, T], BF16)
            knb = ph.sb("knb", [128, T], BF16)
            kstg = ph.sb("kstg", [128, NCH, 128], BF16)
            zt = ph.sb("zt", [16, T], F32)
            gw = ph.sb("gw", [16, 1024], F32)
            gb = ph.sb("gb", [128, 2, 8], F32)
            negb = ph.sb("negb", [128, 2, 8], F32)
            ebe = ph.sb("ebe", [128, NCH], F32)
            pg = [ph.ps("pg%d" % k) for k in range(2)]
            pt = [ph.ps("pt%d" % k, [128, 512], BF16) for k in range(2)]
            ph.dma("sp", gb[:], gb_d[:, j], writes=["gb"], key="gb")
            ph.op("dve", lambda e: e.tensor_scalar_mul(out=negb[:], in0=gb[:], scalar1=-1.0), ["gb"], ["negb"])
            cnt = defaultdict(int)

            def do_cc(d, cc):
                h, kc = cc // 2, cc % 2
                for bi, (t0, N) in enumerate(tb):
                    pb = bi % 2
                    ph.op("pe", lambda e, t0=t0, N=N, pb=pb: e.matmul(
                        pg[pb][:, 0:N], lhsT=gw[0:16, cc * 128:(cc + 1) * 128], rhs=zt[0:16, t0:t0 + N],
                        start=True, stop=True), ["gw", "zt"], [("pg", pb)])
                    ph.op("act", lambda e, t0=t0, N=N, pb=pb: e.activation(
                        out=spb[:, t0:t0 + N], in_=pg[pb][:, 0:N], func=AF.Exp, scale=-1.0,
                        bias=negb[:, d, cc:cc + 1]), [("pg", pb), "negb"], ["spb"])
                ph.op("act", lambda e: e.activation(out=spb[:], in_=spb[:], func=AF.Ln, bias=1.0), ["spb"], ["spb"])
                if d == 0:
                    ph.op("dve", lambda e: e.tensor_tensor_scan(out=bcs[:], data0=mask[:], data1=spb[:], initial=0.0,
                                                                op0=ALU.mult, op1=ALU.add), ["mask", "spb"], ["bcs"])
                    bend = bcs[:, 127::128]
                else:
                    ph.op("dve", lambda e: e.tensor_tensor_scan(out=bcs[:, ::-1], data0=mask[:, ::-1],
                                                                data1=spb[:, ::-1], initial=0.0,
                                                                op0=ALU.mult, op1=ALU.add), ["mask", "spb"], ["bcs"])
                    bend = bcs[:, 0::128]
                ph.dma("sp", qv[:], projB_d[cc * 128:(cc + 1) * 128, :], writes=["qv"], key="qv")
                ph.dma("sp", kv[:], projB_d[1024 + cc * 128:1024 + (cc + 1) * 128, :], writes=["kv"], key="kv")
                ph.op("act", lambda e: e.activation(out=E[:], in_=bcs[:], func=AF.Exp, scale=-1.0 / 16), ["bcs"], ["E"])
                ph.op("dve", lambda e: e.scalar_tensor_tensor(out=qeb[:], in0=qv[:], scalar=1.0 / 16, in1=E[:],
                                                              op0=ALU.mult, op1=ALU.mult), ["qv", "E"], ["qeb"])
                ph.dma("pool", qe_d[d, cc * 128:(cc + 1) * 128, :], qeb[:], reads=["qeb"], key="qeb")
                ph.op("act", lambda e: e.activation(out=E[:], in_=bcs[:], func=AF.Exp, scale=1.0 / 16), ["bcs", "E"], ["E"])
                ph.op("dve", lambda e: e.tensor_tensor(out=keb[:], in0=kv[:], in1=E[:], op=ALU.mult),
                      ["kv", "E"], ["keb"])
                ph.dma("pool", ke_d[d, cc * 128:(cc + 1) * 128, :], keb[:], reads=["keb"], key="keb")
                ph.op("dve", lambda e: e.tensor_tensor(
                    out=tmp[:].rearrange("p (c t) -> p c t", t=128), in0=bcs[:].rearrange("p (c t) -> p c t", t=128),
                    in1=bend.unsqueeze(2).broadcast_to([128, NCH, 128]), op=ALU.subtract), ["bcs"], ["tmp"])
                ph.op("act", lambda e: e.activation(out=tmp[:], in_=tmp[:], func=AF.Exp, scale=1.0 / 16), ["tmp"], ["tmp"])
                ph.op("dve", lambda e: e.tensor_tensor(out=knb[:], in0=kv[:], in1=tmp[:], op=ALU.mult),
                      ["kv", "tmp"], ["knb"])
                ph.op("act", lambda e: e.activation(out=ebe[:], in_=bend, func=AF.Exp, scale=-1.0 / 16), ["bcs"], ["ebe"])
                ph.dma("pool", ebend_d[d, cc], ebe[:], reads=["ebe"], key="ebe")
                for c0 in range(0, NCH, 4):
                    nq = min(4, NCH - c0)
                    pb = cnt["pt"] % 2
                    cnt["pt"] += 1

                    def trp(e, c0=c0, nq=nq, pb=pb):
                        for q in range(nq):
                            rr = e.transpose(pt[pb][:, q * 128:(q + 1) * 128],
                                             knb[:, (c0 + q) * 128:(c0 + q + 1) * 128], identbf[:])
                        return rr
                    ph.op("pe", trp, ["knb"], [("pt", pb)])
                    ph.op("act", lambda e, c0=c0, nq=nq, pb=pb: e.activation(
                        out=kstg[:, c0:c0 + nq, :], in_=pt[pb][:, 0:nq * 128].rearrange("p (q k) -> p q k", k=128),
                        func=AF.Copy), [("pt", pb)], ["kstg"])
                ph.dma("pool", kend_d[d, h][:, kc * 128:(kc + 1) * 128].rearrange("(c s) k -> s c k", s=128),
                       kstg[:], reads=["kstg"], key="kstg")

            for d in range(2):
                ph.dma("sp", zt[:], projB_d[4096 + d * 16:4096 + (d + 1) * 16, :], writes=["zt"], key="zt")
                ph.dma("sp", gw[:], gw2_d[j, d], writes=["gw"], key="gw")
                ph.op("pool", lambda e: e.memset(mask[:], 1.0), [], ["mask"])
                zc = 0 if d == 0 else 127
                ph.op("pool", lambda e, zc=zc: e.memset(mask[:, zc::128], 0.0), ["mask"], ["mask"])
                for cc in range(8):
                    do_cc(d, cc)
                    emit_conv(ph, li + 1, 5)
            emit_conv(ph, li + 1, 10 ** 6)
        stats.append(("gp", ph.stats))

    def gla_rec(li):
        j = li // 2
        need_ctx = li < DEPTH - 1
        with Phase(nc, "gr") as ph:
            S32 = [ph.sb("S32_%d" % h, [128, 2, 512], F32) for h in range(4)]
            Sbf = [ph.sb("Sbf_%d" % h, [128, 2, 512], BF16) for h in range(4)]
            qes = [ph.sb("qes%d" % h, [128, 2, 512], BF16) for h in range(4)]
            kes = [ph.sb("kes%d" % h, [128, 2, 512], BF16) for h in range(4)]
            kns = [ph.sb("kns%d" % h, [128, 4, 256], BF16) for h in range(4)]
            vs = [ph.sb("vs%d" % h, [128, 4, 512], BF16) for h in range(4)]
            osb = [ph.sb("osb%d" % h, [128, 4, 512], F32) for h in range(4)]
            ebs = [ph.sb("ebs%d" % h, [128, 2, NCH], F32) for h in range(4)]
            atts = [ph.sb("att%d" % h, [128, 128], BF16) for h in range(4)]
            o0s = ph.sb("o0s", [128, 4, 512], F32)
            rg = ph.sb("rg", [128, 4, 512], F32)
            sqb = ph.sb("sqb", [128, 4, 512], BF16)
            rstd = ph.sb("rstd", [128, 512], F32)
            tm1 = ph.sb("tm1", [128, 512], F32)
            ystg = ph.sb("ystg", [128, 4, 512], BF16)
            tri = ph.sb("tri", [128, 2, 128], F32)
            odg = ph.sb("odg", [128, 4], F32)
            patt = [ph.ps("patt%d" % k) for k in range(2)]
            po = [ph.ps("po%d" % k) for k in range(2)]
            psu = [ph.ps("psu%d" % k) for k in range(4)]
            ph.dma("sp", tri[:], tri_d.ap().rearrange("d s t -> s d t"), writes=["tri"], key="tri")
            ph.dma("sp", odg[:], odg_d[:, j], writes=["odg"], key="odg")
            cnt = defaultdict(int)

            def nxt(name, mod):
                v = cnt[name] % mod
                cnt[name] += 1
                return v

            def chunk_step(d, h, c, lc):
                pa_ = nxt("patt", 2)
                H = ("h", h)

                def att(e):
                    e.matmul(patt[pa_][:, 0:128], lhsT=kes[h][:, 0, lc * 128:(lc + 1) * 128],
                             rhs=qes[h][:, 0, lc * 128:(lc + 1) * 128], start=True, stop=False)
                    return e.matmul(patt[pa_][:, 0:128], lhsT=kes[h][:, 1, lc * 128:(lc + 1) * 128],
                                    rhs=qes[h][:, 1, lc * 128:(lc + 1) * 128], start=False, stop=True)
                ph.op("pe", att, [("kes", h), ("qes", h)], [("patt", pa_)])
                ph.op("dve", lambda e: e.tensor_tensor(out=atts[h][:], in0=patt[pa_][:, 0:128], in1=tri[:, d, :],
                                                       op=ALU.mult), [("patt", pa_), "tri"], [("att", h)])
                pb_ = nxt("po", 2)

                def omm(e):
                    for m in range(4):
                        e.matmul(po[pb_][:, m * 128:(m + 1) * 128], lhsT=Sbf[h][:, 0, m * 128:(m + 1) * 128],
                                 rhs=qes[h][:, 0, lc * 128:(lc + 1) * 128], start=True, stop=False)
                        e.matmul(po[pb_][:, m * 128:(m + 1) * 128], lhsT=Sbf[h][:, 1, m * 128:(m + 1) * 128],
                                 rhs=qes[h][:, 1, lc * 128:(lc + 1) * 128], start=False, stop=False)
                        rr = e.matmul(po[pb_][:, m * 128:(m + 1) * 128], lhsT=vs[h][:, lc, m * 128:(m + 1) * 128],
                                      rhs=atts[h][:], start=False, stop=True)
                    return rr
                ph.op("pe", omm, [("Sbf", h), ("qes", h), ("vs", h), ("att", h)], [("po", pb_)])
                ph.op("act", lambda e: e.activation(
                    out=osb[h][:, :, lc * 128:(lc + 1) * 128],
                    in_=po[pb_][:, :].rearrange("p (m t) -> p m t", t=128), func=AF.Copy), [("po", pb_)], [("osb", h)])
                ps_ = nxt("psu", 2)

                def smm(e):
                    e.matmul(psu[2 * ps_][:, :], lhsT=kns[h][:, lc, 0:128], rhs=vs[h][:, lc, :], start=True, stop=True)
                    return e.matmul(psu[2 * ps_ + 1][:, :], lhsT=kns[h][:, lc, 128:256], rhs=vs[h][:, lc, :],
                                    start=True, stop=True)
                ph.op("pe", smm, [("kns", h), ("vs", h)], [("psu", ps_)])
                for kc in range(2):
                    ph.op("dve", lambda e, kc=kc: e.scalar_tensor_tensor(
                        out=S32[h][:, kc, :], in0=S32[h][:, kc, :], scalar=ebs[h][:, kc, c:c + 1],
                        in1=psu[2 * ps_ + kc][:, :], op0=ALU.mult, op1=ALU.add),
                        [("S32", h, kc), ("psu", ps_), ("ebs", h)], [("S32", h, kc)])
                ph.op("act", lambda e: e.activation(out=Sbf[h][:, :, :], in_=S32[h][:, :, :], func=AF.Copy),
                      [("S32", h, 0), ("S32", h, 1)], [("Sbf", h)])

            def epilogue(h, t0, N):
                nch = N // 128
                ph.dma("sp", o0s[:, :, 0:N], o0_d[h * 512:(h + 1) * 512, t0:t0 + N].rearrange("(m p) t -> p m t", p=128),
                       reads=[("o0d", h, t0)], writes=["o0s"], key="o0s")
                ph.dma("sp", rg[:, :, 0:N],
                       projB_d[2048 + h * 512:2048 + (h + 1) * 512, t0:t0 + N].rearrange("(m p) t -> p m t", p=128),
                       writes=["rg"], key="rg")
                ph.op("dve", lambda e: e.tensor_tensor(out=osb[h][:, :, 0:N], in0=osb[h][:, :, 0:N], in1=o0s[:, :, 0:N],
                                                       op=ALU.add), [("osb", h), "o0s"], [("osb", h)])
                ph.op("act", lambda e: e.activation(out=sqb[:, :, 0:N], in_=osb[h][:, :, 0:N], func=AF.Square),
                      [("osb", h)], ["sqb"])
                pa_ = nxt("patt", 2)

                def mm(e):
                    for m in range(4):
                        rr = e.matmul(patt[pa_][:, 0:N], lhsT=onesbf[:], rhs=sqb[:, m, 0:N], start=(m == 0), stop=(m == 3))
                    return rr
                ph.op("pe", mm, ["sqb"], [("patt", pa_)])
                ph.op("act", lambda e: e.activation(out=rstd[:, 0:N], in_=patt[pa_][:, 0:N], func=AF.Sqrt,
                                                    scale=1.0 / 512, bias=epsc[:, 0:1]), [("patt", pa_)], ["rstd"])
                ph.op("dve", lambda e: e.reciprocal(out=rstd[:, 0:N], in_=rstd[:, 0:N]), ["rstd"], ["rstd"])
                ph.op("act", lambda e: e.activation(out=rg[:, :, 0:N], in_=rg[:, :, 0:N], func=AF.Silu), ["rg"], ["rg"])
                for m in range(4):
                    ph.op("dve", lambda e, m=m: e.tensor_tensor(out=tm1[:, 0:N], in0=osb[h][:, m, 0:N], in1=rstd[:, 0:N],
                                                                op=ALU.mult), [("osb", h), "rstd"], ["tm1"])
                    ph.op("act", lambda e, m=m: e.activation(out=tm1[:, 0:N], in_=tm1[:, 0:N], func=AF.Identity,
                                                             scale=odg[:, m:m + 1]), ["tm1", "odg"], ["tm1"])
                    ph.op("dve", lambda e, m=m: e.tensor_tensor(out=ystg[:, m, 0:N], in0=tm1[:, 0:N], in1=rg[:, m, 0:N],
                                                                op=ALU.mult), ["tm1", "rg"], ["ystg"])
                ph.dma("pool", yT_d[h * 512:(h + 1) * 512, t0:t0 + N].rearrange("(m p) t -> p m t", p=128),
                       ystg[:, :, 0:N], reads=["ystg"], key="ystg")

            for d in range(2):
                for h in range(4):
                    ph.op("dve", lambda e, h=h: e.memset(S32[h][:], 0.0), [], [("S32", h, 0), ("S32", h, 1)])
                    ph.op("pool", lambda e, h=h: e.memset(Sbf[h][:], 0.0), [], [("Sbf", h)])
                    ph.dma("sp", ebs[h][:], ebend_d[d, 2 * h:2 * h + 2].rearrange("k p c -> p k c"),
                           writes=[("ebs", h)], key=("ebs", h))
                lat = [b for b in cfg.blocks if not b[2]]
                order = [cfg.blocks[0]] + (lat if d == 0 else lat[::-1])
                for (t0, N, is_ctx) in order:
                    nch = N // 128
                    for h in range(4):
                        ph.dma("sp", qes[h][:, :, 0:N],
                               qe_d[d, h * 256:(h + 1) * 256, t0:t0 + N].rearrange("(k p) t -> p k t", p=128),
                               writes=[("qes", h)], key=("qes", h))
                        ph.dma("sp", kes[h][:, :, 0:N],
                               ke_d[d, h * 256:(h + 1) * 256, t0:t0 + N].rearrange("(k p) t -> p k t", p=128),
                               writes=[("kes", h)], key=("kes", h))
                        ph.dma("sp", kns[h][:, 0:nch, :],
                               kend_d[d, h][t0:t0 + N, :].rearrange("(c s) k -> s c k", s=128),
                               writes=[("kns", h)], key=("kns", h))
                        ph.dma("sp", vs[h][:, 0:nch, :],
                               vTok_d[t0:t0 + N, h * 512:(h + 1) * 512].rearrange("(c s) v -> s c v", s=128),
                               writes=[("vs", h)], key=("vs", h))
                    lcs = list(range(nch)) if d == 0 else list(range(nch))[::-1]
                    for lc in lcs:
                        for h in range(4):
                            chunk_step(d, h, t0 // 128 + lc, lc)
                    if is_ctx and not need_ctx:
                        continue
                    for h in range(4):
                        if d == 0:
                            ph.dma("pool", o0_d[h * 512:(h + 1) * 512, t0:t0 + N].rearrange("(m p) t -> p m t", p=128),
                                   osb[h][:, :, 0:N], reads=[("osb", h)], writes=[("o0d", h, t0)], key=("osb", h))
                        else:
                            epilogue(h, t0, N)
        stats.append(("gr", ph.stats))

    plan = [lambda: token_pass(None, 0)]
    for li in range(DEPTH):
        if li % 2 == 0:
            plan.append(lambda li=li: attention(li))
            plan.append(lambda li=li: rglru(li))
        else:
            plan.append(lambda li=li: gla_prep(li))
            plan.append(lambda li=li: gla_rec(li))
        plan.append(lambda li=li: token_pass(li, li + 1 if li + 1 < DEPTH else None))
    stop = getattr(cfg, "stop", None)
    for k, th in enumerate(plan):
        if stop is not None and k >= stop:
            break
        th()
    if dbg_d:
        with Phase(nc, "dbg") as ph:
            for nm, (dst, src) in dbg_d.items():
                ph.dma("sp", dst.ap(), src.ap(), key=("dbg", nm))
    return nc, stats


def _pc(v):
    v = np.asarray(v, np.float32)
    n = v.shape[-1] // 128
    w = v.reshape(v.shape[:-1] + (n, 128))
    return np.ascontiguousarray(np.moveaxis(w, -1, 0))


def _consts(L):
    T = CTX + L
    ident = np.eye(128, dtype=np.float32)
    R = np.zeros((128, 128), np.float32)
    for base in (0, 64):
        for g in (0, 32):
            for i in range(16):
                R[base + g + i, base + g + 16 + i] = -1.0
                R[base + g + 16 + i, base + g + i] = 1.0
    RT = np.ascontiguousarray(R.T)
    inv = (10000.0 ** (-np.arange(16, dtype=np.float32) / 16)).astype(np.float32)
    rows = L // GRID_W
    r = np.repeat(np.arange(rows, dtype=np.float32), GRID_W)
    col = np.tile(np.arange(GRID_W, dtype=np.float32), rows)
    ang_r = (r[:, None] * inv).astype(np.float32)
    ang_c = (col[:, None] * inv).astype(np.float32)
    ang64 = np.concatenate([ang_r, ang_r, ang_c, ang_c], axis=1)
    cos = np.cos(ang64).T.astype(np.float32)
    sin = np.sin(ang64).T.astype(np.float32)
    cosf = np.ones((128, T), np.float32)
    sinf = np.zeros((128, T), np.float32)
    cosf[0:64, CTX:] = cos
    cosf[64:128, CTX:] = cos
    sinf[0:64, CTX:] = sin
    sinf[64:128, CTX:] = sin
    rope = np.stack([cosf * 0.125, sinf * 0.125, cosf, sinf]).astype(np.float32)
    s = np.arange(128)
    tri = np.stack([(s[:, None] <= s[None, :]), (s[:, None] >= s[None, :])]).astype(np.float32)
    return dict(ident=ident, RT=RT, rope=rope, tri=tri)


def make_in_maps(cfg, inp):
    depth, NE, NO = cfg.depth, cfg.NE, cfg.NO
    B = inp["x"].shape[0]
    f = lambda a: np.ascontiguousarray(np.asarray(a, np.float32))
    common = dict(
        ada_w=f(inp["ada_w"]), ada_b_l=_pc(inp["ada_b"]),
        gains=_pc(np.concatenate([np.stack([inp["norm1_g"][i], inp["norm2_g"][i]]) for i in range(depth)]
                                 + [np.asarray(inp["final_g"])[None]], axis=0)),
        mlp_w1=f(inp["mlp_w1"]), mlp_w2=f(inp["mlp_w2"]),
        ev_w_in=f(inp["ev_w_in"]), ev_w_out=f(inp["ev_w_out"]),
        lamv=f(np.stack([inp["ev_lambda_q1"], inp["ev_lambda_k1"], inp["ev_lambda_q2"], inp["ev_lambda_k2"]],
                        axis=1))[None],
        subln=np.ascontiguousarray(f(inp["ev_subln_g"]).T),
        convw=_pc(np.concatenate([f(inp["ev_conv_w"]), f(inp["ev_conv_b"])[:, None, :]], axis=1)).transpose(0, 1, 3, 2).copy(),
        lru_wa=f(inp["ev_lru_wa"]), lru_wx=f(inp["ev_lru_wx"]),
        lru_v=_pc(np.stack([f(inp["ev_lru_ba"]), f(inp["ev_lru_bx"]), f(inp["ev_lru_lam"])], axis=2)
                  ).transpose(0, 1, 2, 4, 3).copy(),
    )
    if NO > 0:
        common.update(od_w_in=f(inp["od_w_in"]), od_w_out=f(inp["od_w_out"]), gate_w2=f(inp["od_gate_w2"]),
                      gate_b_l=_pc(inp["od_gate_b"]), od_norm_g_l=_pc(inp["od_norm_g"]))
    else:
        common.update(od_w_in=np.zeros((1, D, OD_IN), np.float32), od_w_out=np.zeros((1, D, D), np.float32),
                      gate_w2=np.zeros((1, 2, 16, 1024), np.float32), gate_b_l=np.zeros((128, 1, 2, 8), np.float32),
                      od_norm_g_l=np.zeros((128, 1, 4), np.float32))
    common.update(_consts(cfg.L))
    maps = []
    for b in range(B):
        m = dict(common)
        m["x"] = f(inp["x"][b])
        m["ctx"] = f(inp["ctx"][b])
        m["svec"] = np.ascontiguousarray(np.stack([_pc(inp["c"][b]), _pc(inp["c_ctx"])], axis=-1))
        maps.append(m)
    return maps


def run(cfg, inp, dbg=None):
    nc, stats = build_program(cfg, dbg=dbg)
    maps = make_in_maps(cfg, inp)
    res = run_bass_kernel_spmd(nc, maps, core_ids=list(range(len(maps))))
    return res, stats


ACTIVE = (0, 1, 4, 5)


def kernel(**inputs):
    cfg = Cfg(4096, 4)
    nc, _ = build_program(cfg)
    maps = make_in_maps(cfg, inputs)
    zero = dict(maps[0])
    zero["x"] = np.zeros_like(maps[0]["x"])
    zero["ctx"] = np.zeros_like(maps[0]["ctx"])
    zero["svec"] = np.zeros_like(maps[0]["svec"])
    full = [zero] * 8
    for b, c in enumerate(ACTIVE):
        full[c] = maps[b]
    res = run_bass_kernel_spmd(nc, full, core_ids=list(range(8)))
    return np.stack([np.asarray(res.results[c]["out"], np.float32) for c in ACTIVE], axis=0)
```
